# Optimizing a Trainium2 kernel written in Bass

```python
import math
import jax
import jax.numpy as jnp
from jax import lax
import numpy as np

D_MODEL = 2048
BATCH = 4
SEQ = 4096
DEPTH = 2

GRID_W = 64
CTX_LEN = 256
N_EVEN = (DEPTH + 1) // 2
N_ODD = DEPTH // 2
MOD_CHUNKS = 6
FFN_HIDDEN = 4 * D_MODEL
NORM_EPS = 1e-6
SCAN_CHUNK = 128

S5_WIDTH = D_MODEL // 2
S5_P = 16
S5_G = S5_WIDTH // S5_P
S5_N = 64
DT_MIN = 0.001
DT_MAX = 0.1
LAMBDA_RE_MAX = -1e-4
RET_DK = 256
RET_H = (D_MODEL // 2) // RET_DK
RET_DV = (D_MODEL // 2) // RET_H
RET_DECAY_BASE = -5.0
EVEN_IN = S5_WIDTH + 2 * RET_H * RET_DK + 2 * RET_H * RET_DV
EVEN_CAT = S5_WIDTH + RET_H * RET_DV

ATT_HD = 64
ATT_H = D_MODEL // ATT_HD
ATT_KVH = ATT_H // 8
ODD_IN = (ATT_H + 2 * ATT_KVH) * ATT_HD
WINDOW = 128
ATT_BLOCK = 128
ROPE_BASE = 10000.0
NEG_INF = -1e30

kernel_name = 'hybrid_s5_retention_swa_dit'


def rms_norm(x, g):
    xf = x.astype(jnp.float32)
    y = xf * lax.rsqrt(jnp.mean(xf * xf, axis=-1, keepdims=True) + NORM_EPS)
    return (y * g.astype(jnp.float32)).astype(x.dtype)


def head_rms(o):
    of = o.astype(jnp.float32)
    return of * lax.rsqrt(jnp.mean(of * of, axis=-1, keepdims=True) + NORM_EPS)


def ada_modulation(cvec, w, b):
    m = jax.nn.silu(cvec) @ w + b
    return jnp.split(m, MOD_CHUNKS, axis=-1)


def modulate(h, shift, scale):
    return h * (1.0 + scale[:, None, :]) + shift[:, None, :]


def gated_residual(x, out, g_post, gate):
    return x + gate[:, None, :] * rms_norm(out, g_post)


def sq_relu_mlp(h, w1, w2):
    a = jax.nn.relu(h @ w1)
    return (a * a) @ w2


def _flip(t):
    return jnp.flip(t, axis=1)


def _identity(t):
    return t


def _linear_recurrence_combine(left, right):
    a1r, a1i, b1r, b1i = left
    a2r, a2i, b2r, b2i = right
    ar = a2r * a1r - a2i * a1i
    ai = a2r * a1i + a2i * a1r
    br = a2r * b1r - a2i * b1i + b2r
    bi = a2r * b1i + a2i * b1r + b2i
    return ar, ai, br, bi


def s5_scan(u, h0, lam_re, lam_im, log_dt, b_re, b_im, c_re, c_im):
    f32 = jnp.float32
    bsz, n, g, p = u.shape
    t = SCAN_CHUNK
    nc = n // t
    lr = jnp.minimum(lam_re.astype(f32), LAMBDA_RE_MAX)
    li = lam_im.astype(f32)
    dt = jnp.exp(log_dt.astype(f32))[:, None]
    zr, zi = lr * dt, li * dt
    ab_mag = jnp.exp(zr)
    ab_re, ab_im = ab_mag * jnp.cos(zi), ab_mag * jnp.sin(zi)
    den = lr * lr + li * li
    nr = ab_re - 1.0
    f_re = (nr * lr + ab_im * li) / den
    f_im = (ab_im * lr - nr * li) / den
    br, bi = b_re.astype(f32), b_im.astype(f32)
    bb_re = f_re[..., None] * br - f_im[..., None] * bi
    bb_im = f_re[..., None] * bi + f_im[..., None] * br
    steps = jnp.arange(1, t + 1, dtype=f32)[:, None, None]
    pw_mag = jnp.exp(steps * zr)
    pw_re, pw_im = pw_mag * jnp.cos(steps * zi), pw_mag * jnp.sin(steps * zi)
    a_re = jnp.broadcast_to(ab_re, (bsz, t) + ab_re.shape)
    a_im = jnp.broadcast_to(ab_im, (bsz, t) + ab_im.shape)
    cr, ci = c_re.astype(f32), c_im.astype(f32)

    def step(h, u_c):
        hr, hi = h
        bu_re = jnp.einsum('btgp,gnp->btgn', u_c, bb_re)
        bu_im = jnp.einsum('btgp,gnp->btgn', u_c, bb_im)
        _, _, sr, si = lax.associative_scan(_linear_recurrence_combine, (a_re, a_im, bu_re, bu_im), axis=1)
        st_re = sr + pw_re * hr[:, None] - pw_im * hi[:, None]
        st_im = si + pw_re * hi[:, None] + pw_im * hr[:, None]
        y = jnp.einsum('btgn,gpn->btgp', st_re, cr) - jnp.einsum('btgn,gpn->btgp', st_im, ci)
        return (st_re[:, -1], st_im[:, -1]), y

    u_chunks = u.astype(f32).reshape(bsz, nc, t, g, p).swapaxes(0, 1)
    h_last, y = lax.scan(step, h0, u_chunks)
    return h_last, y.swapaxes(0, 1).reshape(bsz, n, g, p)


def s5_bidirectional(u_ctx, u_lat, lam_re, lam_im, log_dt, b_re, b_im, c_re, c_im):
    bsz = u_lat.shape[0]
    ys_c, ys_l = [], []
    for direction in range(2):
        rev = _flip if direction == 1 else _identity
        prm = (lam_re[direction], lam_im[direction], log_dt[direction],
               b_re[direction], b_im[direction], c_re[direction], c_im[direction])
        h0 = (jnp.zeros((bsz, S5_G, S5_N), jnp.float32), jnp.zeros((bsz, S5_G, S5_N), jnp.float32))
        h_ctx, yc = s5_scan(rev(u_ctx), h0, *prm)
        _, yl = s5_scan(rev(u_lat), h_ctx, *prm)
        ys_c.append(rev(yc))
        ys_l.append(rev(yl))
    return ys_c[0] + ys_c[1], ys_l[0] + ys_l[1]


def retention_log_decay(direction):
    e = RET_DECAY_BASE - (2.0 * jnp.arange(RET_H, dtype=jnp.float32) + direction)
    return jnp.log1p(-jnp.exp2(e))


def retention_scan(q, k, v, r0, log_g):
    f32 = jnp.float32
    bsz, n = q.shape[:2]
    t = SCAN_CHUNK
    nc = n // t
    pos = jnp.arange(t, dtype=f32)
    diff = pos[:, None] - pos[None, :]
    lower = diff >= 0
    decay_in = jnp.where(lower[None], jnp.exp(jnp.maximum(diff, 0.0)[None] * log_g[:, None, None]), 0.0)
    q_dec = jnp.exp((pos[:, None] + 1.0) * log_g[None, :])
    k_dec = jnp.exp((t - 1.0 - pos)[:, None] * log_g[None, :])
    chunk_dec = jnp.exp(t * log_g)

    def to_chunks(a):
        return a.astype(f32).reshape((bsz, nc, t) + a.shape[2:]).swapaxes(0, 1)

    def step(r, xs):
        qc, kc, vc = xs
        s = jnp.einsum('bnhd,bmhd->bhnm', qc, kc) * decay_in[None]
        inner = jnp.einsum('bhnm,bmhe->bnhe', s, vc)
        cross = jnp.einsum('bnhd,bhde->bnhe', qc, r) * q_dec[None, :, :, None]
        r_new = r * chunk_dec[None, :, None, None] + jnp.einsum('bmhd,bmhe->bhde', kc * k_dec[None, :, :, None], vc)
        return r_new, inner + cross

    r_last, o = lax.scan(step, r0, (to_chunks(q), to_chunks(k), to_chunks(v)))
    return r_last, o.swapaxes(0, 1).reshape(bsz, n, RET_H, RET_DV)


def retention_bidirectional(q_c, k_c, v_c, q_l, k_l, v_l):
    bsz = q_l.shape[0]
    os_c, os_l = [], []
    for direction in range(2):
        rev = _flip if direction == 1 else _identity
        log_g = retention_log_decay(direction)
        r0 = jnp.zeros((bsz, RET_H, RET_DK, RET_DV), jnp.float32)
        r_ctx, oc = retention_scan(rev(q_c), rev(k_c), rev(v_c), r0, log_g)
        _, ol = retention_scan(rev(q_l), rev(k_l), rev(v_l), r_ctx, log_g)
        os_c.append(rev(oc))
        os_l.append(rev(ol))
    return os_c[0] + os_c[1], os_l[0] + os_l[1]


def s5_retention_mixer(h_ctx, h_lat, w_in, w_out, lam_re, lam_im, log_dt, b_re, b_im,
                       c_re, c_im, d_skip, glu_w, glu_b, need_ctx):
    qk_w = RET_H * RET_DK
    v_w = RET_H * RET_DV
    cuts = [S5_WIDTH, S5_WIDTH + qk_w, S5_WIDTH + 2 * qk_w, S5_WIDTH + 2 * qk_w + v_w]

    def project(h):
        bsz, n, _ = h.shape
        u, q, k, v, g = jnp.split(h @ w_in, cuts, axis=-1)
        return (u, u.reshape(bsz, n, S5_G, S5_P),
                q.reshape(bsz, n, RET_H, RET_DK),
                k.reshape(bsz, n, RET_H, RET_DK) * (RET_DK ** -0.5),
                v.reshape(bsz, n, RET_H, RET_DV), g)

    u_c, u4_c, q_c, k_c, v_c, g_c = project(h_ctx)
    u_l, u4_l, q_l, k_l, v_l, g_l = project(h_lat)
    ys_c, ys_l = s5_bidirectional(u4_c, u4_l, lam_re, lam_im, log_dt, b_re, b_im, c_re, c_im)
    r_c, r_l = retention_bidirectional(q_c, k_c, v_c, q_l, k_l, v_l)

    def merge(y_s5, u, r, g):
        bsz, n = u.shape[:2]
        y = jax.nn.gelu(y_s5.reshape(bsz, n, S5_WIDTH).astype(u.dtype) + d_skip * u)
        s5_out = y * jax.nn.sigmoid(y @ glu_w + glu_b)
        ret_out = head_rms(r).reshape(bsz, n, v_w).astype(g.dtype) * jax.nn.silu(g)
        return jnp.concatenate([s5_out, ret_out], axis=-1) @ w_out

    out_l = merge(ys_l, u_l, r_l, g_l)
    out_c = merge(ys_c, u_c, r_c, g_c) if need_ctx else None
    return out_c, out_l


def axial_rope_tables(n_tokens):
    f32 = jnp.float32
    rows = n_tokens // GRID_W
    row = jnp.repeat(jnp.arange(rows, dtype=f32), GRID_W)
    col = jnp.tile(jnp.arange(GRID_W, dtype=f32), rows)
    n_freq = ATT_HD // 4
    inv_freq = ROPE_BASE ** (-jnp.arange(n_freq, dtype=f32) / n_freq)
    ang = jnp.concatenate([row[:, None] * inv_freq[None], col[:, None] * inv_freq[None]], axis=-1)
    return jnp.cos(ang), jnp.sin(ang)


def apply_rope(x, cos, sin):
    half = ATT_HD // 2
    c = cos[None, :, None, :].astype(x.dtype)
    s = sin[None, :, None, :].astype(x.dtype)
    x1, x2 = x[..., :half], x[..., half:]
    return jnp.concatenate([x1 * c - x2 * s, x2 * c + x1 * s], axis=-1)


def sink_probs(sink_kg, parts):
    lead = parts[0].shape[:-1]
    s_sink = jnp.broadcast_to(sink_kg[None, :, :, None, None], lead + (1,))
    p = jax.nn.softmax(jnp.concatenate([s_sink] + list(parts), axis=-1), axis=-1)
    return p[..., 1:]


def window_attention(h_ctx, h_lat, w_in, w_out, sink, need_ctx):
    f32 = jnp.float32
    bsz, n, _ = h_lat.shape
    n_ctx = h_ctx.shape[1]
    grp = ATT_H // ATT_KVH
    qw = ATT_H * ATT_HD
    kvw = ATT_KVH * ATT_HD
    scale = ATT_HD ** -0.5
    p_l = h_lat @ w_in
    q_l = p_l[..., :qw].reshape(bsz, n, ATT_H, ATT_HD)
    k_l = p_l[..., qw:qw + kvw].reshape(bsz, n, ATT_KVH, ATT_HD)
    v_l = p_l[..., qw + kvw:].reshape(bsz, n, ATT_KVH, ATT_HD)
    cos, sin = axial_rope_tables(n)
    q_l = (apply_rope(q_l, cos, sin) * scale).reshape(bsz, n, ATT_KVH, grp, ATT_HD)
    k_l = apply_rope(k_l, cos, sin)
    p_c = h_ctx @ (w_in if need_ctx else w_in[:, qw:])
    k_c = p_c[..., -2 * kvw:-kvw].reshape(bsz, n_ctx, ATT_KVH, ATT_HD)
    v_c = p_c[..., -kvw:].reshape(bsz, n_ctx, ATT_KVH, ATT_HD)
    sink_kg = sink.astype(f32).reshape(ATT_KVH, grp)

    t = ATT_BLOCK
    band = t + 2 * WINDOW
    pad = ((0, 0), (WINDOW, WINDOW), (0, 0), (0, 0))
    k_pad = jnp.pad(k_l, pad)
    v_pad = jnp.pad(v_l, pad)
    offs_q = jnp.arange(t)
    offs_k = jnp.arange(band)
    in_window = jnp.abs(offs_k[None, :] - WINDOW - offs_q[:, None]) <= WINDOW

    def attend_block(bi):
        start = bi * t
        qb = lax.dynamic_slice_in_dim(q_l, start, t, axis=1)
        kb = lax.dynamic_slice_in_dim(k_pad, start, band, axis=1)
        vb = lax.dynamic_slice_in_dim(v_pad, start, band, axis=1)
        kpos = start - WINDOW + offs_k
        valid = in_window & ((kpos >= 0) & (kpos < n))[None, :]
        s_ctx = jnp.einsum('btkgd,bskd->bkgts', qb, k_c).astype(f32)
        s_loc = jnp.where(valid, jnp.einsum('btkgd,bskd->bkgts', qb, kb).astype(f32), NEG_INF)
        p = sink_probs(sink_kg, [s_ctx, s_loc]).astype(vb.dtype)
        return (jnp.einsum('bkgts,bskd->btkgd', p[..., :n_ctx], v_c)
                + jnp.einsum('bkgts,bskd->btkgd', p[..., n_ctx:], vb))

    o_l = lax.map(attend_block, jnp.arange(n // t))
    out_l = o_l.swapaxes(0, 1).reshape(bsz, n, qw) @ w_out
    out_c = None
    if need_ctx:
        q_c = (p_c[..., :qw] * scale).reshape(bsz, n_ctx, ATT_KVH, grp, ATT_HD)
        s = jnp.einsum('btkgd,bskd->bkgts', q_c, k_c).astype(f32)
        p = sink_probs(sink_kg, [s]).astype(v_c.dtype)
        out_c = jnp.einsum('bkgts,bskd->btkgd', p, v_c).reshape(bsz, n_ctx, qw) @ w_out
    return out_c, out_l


def setup_inputs(seed: int = 0) -> dict:
    key = jax.random.key(seed)
    ks = jax.random.split(key, 24)
    f32 = jnp.float32
    d = D_MODEL

    def nrm(k, shape, s):
        return jax.random.normal(k, shape, f32) * s

    n_idx = jnp.arange(S5_N, dtype=f32)
    s5_shape = (N_EVEN, 2, S5_G, S5_N)
    return {
        'x': nrm(ks[0], (BATCH, SEQ, d), 1.0),
        'c': nrm(ks[1], (BATCH, d), 1.0),
        'ctx': nrm(ks[2], (BATCH, CTX_LEN, d), 1.0),
        'c_ctx': nrm(ks[3], (d,), 1.0),
        'mod_w': nrm(ks[4], (DEPTH, d, MOD_CHUNKS * d), 0.5 * d ** -0.5),
        'mod_b': nrm(ks[5], (DEPTH, MOD_CHUNKS * d), 0.02),
        'norm_g': 1.0 + nrm(ks[6], (DEPTH, 4, d), 0.05),
        'mlp_w1': nrm(ks[7], (DEPTH, d, FFN_HIDDEN), d ** -0.5),
        'mlp_w2': nrm(ks[8], (DEPTH, FFN_HIDDEN, d), FFN_HIDDEN ** -0.5),
        'even_w_in': nrm(ks[9], (N_EVEN, d, EVEN_IN), d ** -0.5),
        'even_w_out': nrm(ks[10], (N_EVEN, EVEN_CAT, d), EVEN_CAT ** -0.5),
        's5_lam_re': -0.5 + nrm(ks[11], s5_shape, 0.01),
        's5_lam_im': math.pi * n_idx + nrm(ks[12], s5_shape, 0.01),
        's5_log_dt': jax.random.uniform(ks[13], (N_EVEN, 2, S5_G), f32, math.log(DT_MIN), math.log(DT_MAX)),
        's5_b_re': nrm(ks[14], (N_EVEN, 2, S5_G, S5_N, S5_P), (2 * S5_P) ** -0.5),
        's5_b_im': nrm(ks[15], (N_EVEN, 2, S5_G, S5_N, S5_P), (2 * S5_P) ** -0.5),
        's5_c_re': nrm(ks[16], (N_EVEN, 2, S5_G, S5_P, S5_N), S5_N ** -0.5),
        's5_c_im': nrm(ks[17], (N_EVEN, 2, S5_G, S5_P, S5_N), S5_N ** -0.5),
        's5_d': nrm(ks[18], (N_EVEN, S5_WIDTH), 1.0),
        's5_glu_w': nrm(ks[19], (N_EVEN, S5_WIDTH, S5_WIDTH), S5_WIDTH ** -0.5),
        's5_glu_b': nrm(ks[20], (N_EVEN, S5_WIDTH), 0.02),
        'odd_w_in': nrm(ks[21], (N_ODD, d, ODD_IN), d ** -0.5),
        'odd_w_out': nrm(ks[22], (N_ODD, ATT_H * ATT_HD, d), (ATT_H * ATT_HD) ** -0.5),
        'odd_sink': nrm(ks[23], (N_ODD, ATT_H), 1.0),
    }


def reference(x, c, ctx, c_ctx, mod_w, mod_b, norm_g, mlp_w1, mlp_w2, even_w_in, even_w_out,
              s5_lam_re, s5_lam_im, s5_log_dt, s5_b_re, s5_b_im, s5_c_re, s5_c_im, s5_d,
              s5_glu_w, s5_glu_b, odd_w_in, odd_w_out, odd_sink):
    xc = ctx
    c_ctx_b = c_ctx[None, :]
    for i in range(DEPTH):
        last = i == DEPTH - 1
        j = i // 2
        sh1, sc1, gt1, sh2, sc2, gt2 = ada_modulation(c, mod_w[i], mod_b[i])
        csh1, csc1, cgt1, csh2, csc2, cgt2 = ada_modulation(c_ctx_b, mod_w[i], mod_b[i])
        g_pre1, g_post1, g_pre2, g_post2 = norm_g[i, 0], norm_g[i, 1], norm_g[i, 2], norm_g[i, 3]

        h_l = modulate(rms_norm(x, g_pre1), sh1, sc1)
        h_c = modulate(rms_norm(xc, g_pre1), csh1, csc1)
        if i % 2 == 0:
            o_c, o_l = s5_retention_mixer(h_c, h_l, even_w_in[j], even_w_out[j],
                                          s5_lam_re[j], s5_lam_im[j], s5_log_dt[j], s5_b_re[j], s5_b_im[j],
                                          s5_c_re[j], s5_c_im[j], s5_d[j], s5_glu_w[j], s5_glu_b[j],
                                          need_ctx=not last)
        else:
            o_c, o_l = window_attention(h_c, h_l, odd_w_in[j], odd_w_out[j], odd_sink[j], need_ctx=not last)
        x = gated_residual(x, o_l, g_post1, gt1)

        m_l = sq_relu_mlp(modulate(rms_norm(x, g_pre2), sh2, sc2), mlp_w1[i], mlp_w2[i])
        x = gated_residual(x, m_l, g_post2, gt2)

        if not last:
            xc = gated_residual(xc, o_c, g_post1, cgt1)
            m_c = sq_relu_mlp(modulate(rms_norm(xc, g_pre2), csh2, csc2), mlp_w1[i], mlp_w2[i])
            xc = gated_residual(xc, m_c, g_post2, cgt2)
    return x
```

```python
import numpy as np
import concourse.bass as bass
import concourse.mybir as mybir
from concourse.bass_utils import run_bass_kernel_spmd

F32 = mybir.dt.float32
BF16 = mybir.dt.bfloat16
AF = mybir.ActivationFunctionType
ALU = mybir.AluOpType

D = 2048
L = 4096
NCTX = 256
OWN = 2048
FULL_LAT = 2176
NFULL = NCTX + FULL_LAT
NALL = NCTX + L
EPS = 1e-6


class Ev:
    __slots__ = ("sem", "val", "closed", "_i", "_q")

    def __init__(self, sem, val):
        self.sem = sem
        self.val = val
        self.closed = True


class Buf:
    def __init__(self, name, ap=None):
        self.name = name
        self.ap = ap
        self.w = None
        self.r = {}
        self.excl = False

    def __getitem__(self, idx):
        return self.ap[idx]


class Ctx:
    def __init__(self, nc):
        self.nc = nc
        self.eng = {"pe": nc.tensor, "act": nc.scalar, "dve": nc.vector, "pool": nc.gpsimd, "sp": nc.sync}
        self.sem = {e: nc.alloc_semaphore("s_" + e) for e in self.eng}
        self.cnt = {e: 0 for e in self.eng}
        self.waited = {e: {} for e in self.eng}
        self.dsems = [nc.alloc_semaphore("d%d" % i) for i in range(90)]
        self.dtot = [0] * len(self.dsems)
        self.dnext = 0
        self.semid = {}
        self.all_events = []
        self.n_inst = 0

    def _sid(self, sem):
        return id(sem)

    def need(self, e, ev, strict=False):
        if ev is None:
            return
        if ev.sem is self.sem[e] and not strict and e == "pe":
            return
        ev.closed = True
        k = self._sid(ev.sem)
        if self.waited[e].get(k, 0) < ev.val:
            self.eng[e].wait_ge(ev.sem, ev.val)
            self.waited[e][k] = ev.val

    def deps(self, e, reads, writes, strict=False):
        for b in reads:
            self.need(e, b.w, strict)
        for b in writes:
            self.need(e, b.w, strict)
            for ev in b.r.values():
                self.need(e, ev, strict)

    def record(self, ev, reads, writes):
        for b in writes:
            b.w = ev
            b.r = {}
        for b in reads:
            b.r[self._sid(ev.sem)] = ev

    def op(self, e, reads, writes, emit):
        writes = list(writes) + [b for b in reads if b.excl]
        reads = [b for b in reads if not b.excl]
        self.deps(e, reads, writes)
        inst = emit()
        self.cnt[e] += 1
        inst.then_inc(self.sem[e], 1)
        ev = Ev(self.sem[e], self.cnt[e])
        self.record(ev, reads, writes)
        self.n_inst += 1
        return ev

    def dma_event(self, q):
        i = self.dnext
        self.dnext = (self.dnext + 1) % len(self.dsems)
        if self.dtot[i] > 0:
            k = self._sid(self.dsems[i])
            if self.waited[q].get(k, 0) < self.dtot[i]:
                self.eng[q].wait_ge(self.dsems[i], self.dtot[i])
                self.waited[q][k] = self.dtot[i]
        ev = Ev(self.dsems[i], self.dtot[i])
        ev.closed = False
        ev._i = i
        ev._q = q
        return ev

    def dma(self, q, out, in_, reads, writes, ev=None, **kw):
        if ev is None:
            ev = self.dma_event(q)
        assert not ev.closed and ev._q == q
        self.deps(q, reads, writes, strict=True)
        self.eng[q].dma_start(out=out, in_=in_, **kw).then_inc(ev.sem, 16)
        self.dtot[ev._i] += 16
        ev.val = self.dtot[ev._i]
        self.record(ev, reads, writes)
        self.all_events.append(ev)
        return ev

    def barrier(self):
        evs = [Ev(self.sem[e], self.cnt[e]) for e in self.eng if self.cnt[e] > 0]
        evs += [Ev(self.dsems[i], self.dtot[i]) for i in range(len(self.dsems)) if self.dtot[i] > 0]
        for e in self.eng:
            for ev in evs:
                self.need(e, ev)
        for ev in self.all_events:
            ev.closed = True
        self.all_events = []


class Prog:
    def __init__(self, debug=()):
        self.nc = nc = bass.Bass("TRN2", target_bir_lowering=False)
        self.C = Ctx(nc)
        self.debug = set(debug)
        self.inputs = {}
        self.outputs = {}
        self.qrr = 0

    def din(self, name, shape, dt=F32):
        b = Buf(name, self.nc.dram_tensor(name, list(shape), dt, kind="ExternalInput").ap())
        self.inputs[name] = b
        return b

    def dout(self, name, shape, dt=F32):
        b = Buf(name, self.nc.dram_tensor(name, list(shape), dt, kind="ExternalOutput").ap())
        self.outputs[name] = b
        return b

    def dscr(self, name, shape, dt):
        if name in self.debug:
            return self.dout(name, shape, dt)
        return Buf(name, self.nc.dram_tensor(name, list(shape), dt, kind="Internal").ap())

    def sb(self, name, shape, dt):
        return Buf(name, self.nc.alloc_sbuf_tensor(name, list(shape), dt).ap())

    def ps(self, name, shape, dt=F32):
        b = Buf(name, self.nc.alloc_psum_tensor(name, list(shape), dt).ap())
        b.excl = True
        return b

    def act(self, out_b, out, in_b, in_, func, bias=None, scale=None, accum=None, extra_r=(), extra_w=(), e="act"):
        nc = self.nc
        kw = {}
        if bias is not None:
            kw["bias"] = bias
        if scale is not None:
            kw["scale"] = scale
        if accum is not None:
            kw["accum_out"] = accum
        return self.C.op("act", [in_b] + list(extra_r), [out_b] + list(extra_w),
                         lambda: nc.scalar.activation(out=out, in_=in_, func=func, **kw))

    def tt(self, e, out_b, out, a_b, a, b_b, b, op, extra_r=()):
        eng = self.C.eng[e]
        return self.C.op(e, [a_b, b_b] + list(extra_r), [out_b], lambda: eng.tensor_tensor(out=out, in0=a, in1=b, op=op))

    def ts(self, e, out_b, out, a_b, a, s1, s2, op0, op1=None, extra_r=()):
        eng = self.C.eng[e]
        if op1 is None:
            return self.C.op(e, [a_b] + list(extra_r), [out_b],
                             lambda: eng.tensor_scalar(out=out, in0=a, scalar1=s1, scalar2=None, op0=op0))
        return self.C.op(e, [a_b] + list(extra_r), [out_b],
                         lambda: eng.tensor_scalar(out=out, in0=a, scalar1=s1, scalar2=s2, op0=op0, op1=op1))

    def stt(self, out_b, out, a_b, a, scalar, b_b, b, op0, op1, extra_r=()):
        nc = self.nc
        return self.C.op("dve", [a_b, b_b] + list(extra_r), [out_b],
                         lambda: nc.vector.scalar_tensor_tensor(out=out, in0=a, scalar=scalar, in1=b, op0=op0, op1=op1))

    def copy(self, e, out_b, out, in_b, in_):
        eng = self.C.eng[e]
        if e == "act":
            return self.C.op(e, [in_b], [out_b], lambda: eng.copy(out=out, in_=in_))
        return self.C.op(e, [in_b], [out_b], lambda: eng.tensor_copy(out=out, in_=in_))

    def memset(self, e, out_b, out, val):
        eng = self.C.eng[e]
        return self.C.op(e, [], [out_b], lambda: eng.memset(out, val))

    def mm(self, out_b, reads, emit):
        return self.C.op("pe", reads, [out_b], emit)

    def dma(self, q, out_b, out, in_b, in_, ev=None, **kw):
        return self.C.dma(q, out, in_, [in_b] if in_b is not None else [], [out_b], ev=ev, **kw)

    def ldq(self):
        self.qrr ^= 1
        return "sp" if self.qrr else "act"


def build(debug=(), stop_after=None):
    P = Prog(debug)
    nc, C = P.nc, P.C

    x_in = P.din("x_loc", [L, D])
    ctx_in = P.din("ctx_loc", [NCTX, D])
    c_col = P.din("c_col", [128, 2, 16])
    mod_w = P.din("mod_w", [2, D, 6 * D])
    modb_col = P.din("modb_col", [128, 2, 4, 16])
    modb_row = P.din("modb_row", [2, 2, D])
    g_col = P.din("g_col", [128, 2, 2, 16])
    g_row = P.din("g_row", [2, 2, D])
    ident_in = P.din("ident", [128, 128])
    w_in0 = P.din("even_w_in", [D, 5120])

    ident_f = P.sb("ident_f", [128, 128], F32)
    ident_b = P.sb("ident_b", [128, 128], BF16)
    P.dma("sp", ident_f, ident_f[:], ident_in, ident_in[:])
    P.copy("dve", ident_b, ident_b[:], ident_f, ident_f[:])
    colA = P.sb("colA", [128, 2, 2, 2, 16], F32)
    colB = P.sb("colB", [128, 2, 2, 2, 16], F32)
    GGS = P.dscr("GGS", [2, 2, 2, D], F32)
    pb = [P.ps("pb%d" % i, [128, 512], F32) for i in range(8)]
    NW = 4
    wpool = []
    wstate = {"i": 0}

    def wload(view):
        wb = wpool[wstate["i"] % len(wpool)]
        wstate["i"] += 1
        kc, n = view.shape[1], view.shape[2]
        q = "pool" if view.dtype == F32 else P.ldq()
        P.dma(q, wb, wb[:, 0:kc, 0:n], None, view)
        return wb

    def precast(src, name):
        shp = list(src.ap.shape)
        dst = P.dscr(name, shp, BF16)
        s2 = src.ap if len(shp) == 2 else src.ap.rearrange("l k n -> (l k) n")
        d2 = dst.ap if len(shp) == 2 else dst.ap.rearrange("l k n -> (l k) n")
        rows, ncols = s2.shape
        rp = max(128, (1 << 20) // ncols)
        for r0 in range(0, rows, rp):
            r1 = min(rows, r0 + rp)
            precast_q.append((dst, d2[r0:r1, :], s2[r0:r1, :]))
        return dst

    precast_q = []

    def precast_step():
        if not precast_q:
            return False
        dst, o_ap, i_ap = precast_q.pop(0)
        P.dma("pool", dst, o_ap, None, i_ap)
        return True

    class Phase:
        def __init__(self):
            self.cms = []

        def sb(self, name, shape, dt):
            wstate["u"] = wstate.get("u", 0) + 1
            name = "%s_u%d" % (name, wstate["u"])
            cm = nc.sbuf_tensor(name, list(shape), dt)
            t = cm.__enter__()
            self.cms.append(cm)
            return Buf(name, t.ap())

        def wpool(self, n=NW):
            wpool[:] = [self.sb("wt%d" % i, [128, 16, 512], BF16) for i in range(n)]

        def close(self):
            C.barrier()
            for cm in reversed(self.cms):
                cm.__exit__(None, None, None)

    def phase0():
        ph = Phase()
        ph.wpool()
        sc_f = ph.sb("sc_f", [128, 2, 16], F32)
        sc_b = ph.sb("sc_b", [128, 2, 16], BF16)
        mb_col = ph.sb("mb_col", [128, 2, 4, 16], F32)
        gcol = ph.sb("gcol", [128, 2, 2, 16], F32)
        rowb = ph.sb("rowb", [2, 2, 2, D], F32)
        rowg = ph.sb("rowg", [2, 2, 2, D], F32)
        rowo = ph.sb("rowo", [2, D], F32)
        cps = pb[0]
        cpsv = pb[0][:, 0:128].rearrange("p (v m s) -> p v m s", v=4, m=16, s=2)
        rps = [pb[1], pb[2]]
        P.dma("sp", sc_f, sc_f[:], c_col, c_col[:])
        P.dma("act", mb_col, mb_col[:], modb_col, modb_col[:])
        P.dma("sp", gcol, gcol[:], g_col, g_col[:])
        P.dma("act", rowb, rowb[:], modb_row, modb_row[:].partition_broadcast(2))
        P.dma("sp", rowg, rowg[:], g_row, g_row[:].partition_broadcast(2))
        P.act(sc_f, sc_f[:], sc_f, sc_f[:], AF.Silu)
        P.copy("dve", sc_b, sc_b[:], sc_f, sc_f[:])
        JV = {0: 0, 1: 1, 3: 2, 4: 3}
        JG = {2: 0, 5: 1}
        rr = 0
        for i in range(2):
            for j in range(6):
                for cb in range(4):
                    wb = wload(mod_w[i, :, j * D + cb * 512: j * D + (cb + 1) * 512].rearrange("(kc p) n -> p kc n", p=128))
                    if j in JV:
                        v = JV[j]

                        def emit(wb=wb, v=v, cb=cb):
                            last = None
                            for m in range(4):
                                mc = cb * 4 + m
                                for kc in range(16):
                                    last = nc.tensor.matmul(cpsv[:, v, mc, :], lhsT=wb[:, kc, m * 128:(m + 1) * 128],
                                                            rhs=sc_b[:, :, kc], start=(kc == 0), stop=(kc == 15))
                            return last
                        P.mm(cps, [wb, sc_b], emit)
                    else:
                        sub = JG[j]
                        rp = rps[rr % 2]
                        rr += 1

                        def emit(wb=wb, rp=rp):
                            last = None
                            for kc in range(16):
                                last = nc.tensor.matmul(rp[0:2, :], lhsT=sc_b[:, :, kc], rhs=wb[:, kc, :],
                                                        start=(kc == 0), stop=(kc == 15))
                            return last
                        P.mm(rp, [wb, sc_b], emit)
                        sl = slice(cb * 512, (cb + 1) * 512)
                        P.tt("dve", rowo, rowo[:, sl], rp, rp[0:2, :], rowb, rowb[:, i, sub, sl], ALU.add)
                        P.tt("dve", rowo, rowo[:, sl], rowo, rowo[:, sl], rowg, rowg[:, i, sub, sl], ALU.mult)
                        if cb == 3:
                            P.dma("sp", GGS, GGS[i, sub], rowo, rowo[:])
            for s in range(2):
                for sub in range(2):
                    vsh, vsc = 2 * sub, 2 * sub + 1
                    P.tt("dve", colB, colB[:, i, s, sub, :], cps, cpsv[:, vsh, :, s], mb_col, mb_col[:, i, vsh, :], ALU.add)
                    P.tt("dve", colA, colA[:, i, s, sub, :], cps, cpsv[:, vsc, :, s], mb_col, mb_col[:, i, vsc, :], ALU.add)
                    P.stt(colA, colA[:, i, s, sub, :], colA, colA[:, i, s, sub, :], 1.0, gcol, gcol[:, i, sub, :], ALU.add, ALU.mult)
        ph.close()

    import os
    if os.environ.get('K_SKIP_P0'):
        P.memset('dve', colA, colA[:], 1.0)
        P.memset('dve', colB, colB[:], 0.0)
    else:
        phase0()
    if stop_after == "p0":
        dbgA = P.dout("dbg_colA", [128, 2, 2, 2, 16])
        dbgB = P.dout("dbg_colB", [128, 2, 2, 2, 16])
        P.dma("sp", dbgA, dbgA[:], colA, colA[:])
        P.dma("sp", dbgB, dbgB[:], colB, colB[:])
        return finish(P)

    def row_rstd(junk, ssb, src_b, src_ap_fn, nch, width):
        for c in range(nch):
            sb_c = src_b[c] if isinstance(src_b, list) else src_b
            P.act(junk, junk[:, 0:width], sb_c, src_ap_fn(c), AF.Square, accum=ssb[:, c:c + 1], extra_w=[ssb])
        P.ts("dve", ssb, ssb[:, 0:nch], ssb, ssb[:, 0:nch], 1.0 / width, EPS, ALU.mult, ALU.add)
        P.act(ssb, ssb[:, 0:nch], ssb, ssb[:, 0:nch], AF.Sqrt)
        P.C.op("dve", [ssb], [ssb], lambda: nc.vector.reciprocal(out=ssb[:, 0:nch], in_=ssb[:, 0:nch]))

    def norm_to_hT(ph_bufs, xt, nch, layer, src, sub, inplace=True, xk=None):
        junk, ssb, hT, hTk = ph_bufs["junk"], ph_bufs["ss"], ph_bufs["hT"], ph_bufs["hTk"]
        T = nch * 128
        xk = xk if xk is not None else [xt] * 4
        row_rstd(junk, ssb, list(xk), lambda c: xt[:, c, :], nch, D)
        if inplace:
            for c in range(nch):
                P.act(xk[c], xt[:, c, :], xk[c], xt[:, c, :], AF.Identity, scale=ssb[:, c:c + 1], extra_r=[ssb])
        else:
            diag = ph_bufs["diag"]
            for c in range(nch):
                P.ts("dve", diag, diag[:, c, :], ident_f, ident_f[:], ssb[:, c:c + 1], None, ALU.mult, extra_r=[ssb])
        for kc in range(16):
            pt = pb[6 + (kc % 2)]

            def emit(kc=kc, pt=pt):
                last = None
                for c in range(nch):
                    if inplace:
                        last = nc.tensor.transpose(out=pt[:, c * 128:(c + 1) * 128], in_=xt[:, c, kc * 128:(kc + 1) * 128], identity=ident_f[:])
                    else:
                        last = nc.tensor.matmul(pt[:, c * 128:(c + 1) * 128], lhsT=xt[:, c, kc * 128:(kc + 1) * 128], rhs=diag[:, c, :],
                                                start=True, stop=True)
                return last
            P.mm(pt, list(dict.fromkeys(xk[:nch])) + [ident_f] + ([] if inplace else [ph_bufs["diag"]]), emit)
            a_ap = colA[:, layer, src, sub, kc:kc + 1]
            b_ap = colB[:, layer, src, sub, kc:kc + 1]
            if kc % 2 == 0:
                P.act(hTk[kc], hT[:, kc, 0:T], pt, pt[:, 0:T], AF.Identity, bias=b_ap, scale=a_ap, extra_r=[colA, colB])
            else:
                P.ts("dve", hTk[kc], hT[:, kc, 0:T], pt, pt[:, 0:T], a_ap, b_ap, ALU.mult, ALU.add, extra_r=[colA, colB])
        return hTk

    UT4 = P.dscr("UT4", [4, 1024, NALL // 4], BF16)
    UTf = P.dscr("UTf", [1024, NFULL], F32)
    QT = P.dscr("QT", [1024, NFULL], BF16)
    KT = P.dscr("KT", [1024, NFULL], BF16)
    GT = P.dscr("GT", [1024, NFULL], F32)
    Ktm = P.dscr("Ktm", [NALL, 1024], BF16)
    Vtm = P.dscr("Vtm", [NALL, 1024], BF16)

    tiles = [("ctx", 0, 2, True, 0)]
    for t0 in range(0, 2048, 512):
        tiles.append(("lat", t0, 4, True, NCTX + t0))
    tiles.append(("lat", 2048, 1, True, NCTX + 2048))
    t0 = 2176
    while t0 < L:
        n = min(4, (L - t0) // 128)
        tiles.append(("lat", t0, n, False, NCTX + t0))
        t0 += n * 128

    def phase1():
        ph = Phase()
        ph.wpool()
        xt = ph.sb("xt", [128, 4, D], F32)
        bufs = {"junk": ph.sb("junk", [128, D], BF16), "ss": ph.sb("ss", [128, 4], F32)}
        hTs = [ph.sb("hT%d" % i, [128, 16, 512], BF16) for i in range(2)]
        hTks = [[Buf("hT%d_%d" % (i, k)) for k in range(16)] for i in range(2)]
        st_b = [ph.sb("stb%d" % i, [128, 4, 512], BF16) for i in range(3)]
        st_f = [ph.sb("stf%d" % i, [128, 4, 512], F32) for i in range(2)]
        cnt = {"ps": 0, "sb": 0, "sf": 0, "ev": 0}
        for ti, (srcname, r0, nch, full, tt0) in enumerate(tiles[:int(os.environ.get('K_P1_TILES', '99'))]):
            T = nch * 128
            src = 1 if srcname == "ctx" else 0
            xin = ctx_in if srcname == "ctx" else x_in
            P.dma(P.ldq(), xt, xt[:, 0:nch, :], xin, xin[r0:r0 + T, :].rearrange("(c p) d -> p c d", p=128))
            bufs["hT"] = hTs[ti % 2]
            bufs["hTk"] = hTks[ti % 2]
            hT = hTs[ti % 2]
            hTk = norm_to_hT(bufs, xt, nch, 0, src, 0)
            for cb in range(int(os.environ.get('K_P1_CBS', '10'))):
                kind = ["u", "q", "k", "v", "g"][cb // 2]
                if not full and kind in ("q", "g"):
                    continue
                wb = wload(w_in0[:, cb * 512:(cb + 1) * 512].rearrange("(kc p) n -> p kc n", p=128))
                half = cb % 2
                if kind in ("u", "q", "g") or (kind == "k" and full):
                    sb_ = st_b[cnt["sb"] % 3]
                    cnt["sb"] += 1
                    sf_ = None
                    if full and kind in ("u", "g"):
                        sf_ = st_f[cnt["sf"] % 2]
                        cnt["sf"] += 1
                    for m in range(4):
                        pp = pb[cnt["ps"] % 6]
                        cnt["ps"] += 1

                        def emit(wb=wb, m=m, pp=pp, hT=hT, T=T):
                            last = None
                            for kc in range(16):
                                last = nc.tensor.matmul(pp[:, 0:T], lhsT=wb[:, kc, m * 128:(m + 1) * 128], rhs=hT[:, kc, 0:T],
                                                        start=(kc == 0), stop=(kc == 15))
                            return last
                        P.mm(pp, [wb] + hTk, emit)
                        if kind == "u":
                            P.copy("act" if cnt["ev"] % 2 else "dve", sb_, sb_[:, m, 0:T].rearrange("p (s j) -> p s j", s=4),
                                   pp, pp[:, 0:T].rearrange("p (j s) -> p s j", s=4))
                            cnt["ev"] += 1
                        elif kind != "g":
                            P.copy("act" if cnt["ev"] % 2 else "dve", sb_, sb_[:, m, 0:T], pp, pp[:, 0:T])
                            cnt["ev"] += 1
                        if sf_ is not None:
                            P.copy("act" if cnt["ev"] % 2 else "dve", sf_, sf_[:, m, 0:T], pp, pp[:, 0:T])
                            cnt["ev"] += 1
                    rows = slice(half * 512, (half + 1) * 512)
                    if kind == "u":
                        for s4 in range(4):
                            P.dma(P.ldq(), UT4, UT4[s4, rows, tt0 // 4:(tt0 + T) // 4].rearrange("(m p) j -> p m j", p=128),
                                  sb_, sb_[:, :, s4 * (T // 4):(s4 + 1) * (T // 4)])
                        if full:
                            P.dma(P.ldq(), UTf, UTf[rows, tt0:tt0 + T].rearrange("(m p) t -> p m t", p=128), sf_, sf_[:, :, 0:T])
                    elif kind == "q":
                        P.dma(P.ldq(), QT, QT[rows, tt0:tt0 + T].rearrange("(m p) t -> p m t", p=128), sb_, sb_[:, :, 0:T])
                    elif kind == "k":
                        P.dma(P.ldq(), KT, KT[rows, tt0:tt0 + T].rearrange("(m p) t -> p m t", p=128), sb_, sb_[:, :, 0:T])
                    elif kind == "g":
                        P.dma(P.ldq(), GT, GT[rows, tt0:tt0 + T].rearrange("(m p) t -> p m t", p=128), sf_, sf_[:, :, 0:T])
                if kind in ("k", "v"):
                    sb_ = st_b[cnt["sb"] % 3]
                    cnt["sb"] += 1
                    for c in range(nch):
                        pp = pb[cnt["ps"] % 6]
                        cnt["ps"] += 1

                        def emit(wb=wb, c=c, pp=pp, hT=hT):
                            last = None
                            for kc in range(16):
                                last = nc.tensor.matmul(pp[:, :], lhsT=hT[:, kc, c * 128:(c + 1) * 128], rhs=wb[:, kc, :],
                                                        start=(kc == 0), stop=(kc == 15))
                            return last
                        P.mm(pp, [wb] + hTk, emit)
                        P.copy("act" if cnt["ev"] % 2 else "dve", sb_, sb_[:, c, :], pp, pp[:, :])
                        cnt["ev"] += 1
                    dst = Ktm if kind == "k" else Vtm
                    P.dma(P.ldq(), dst, dst[tt0:tt0 + T, half * 512:(half + 1) * 512].rearrange("(c p) n -> p c n", p=128), sb_, sb_[:, 0:nch, :])
        ph.close()

    if not os.environ.get('K_SKIP_P1'):
        phase1()
    if stop_after == "p1":
        return finish(P)

    ret_dmask = P.din("ret_dmask", [128, 2, 4, 128])
    ret_qdec = P.din("ret_qdec", [2, 4, 128])
    ret_kdec = P.din("ret_kdec", [128, 2, 4])
    ret_cdec = P.din("ret_cdec", [128, 2, 4])
    OA = P.dscr("OA", [1024, NFULL], F32)
    OT = P.dscr("OT", [1024, NFULL], F32)

    orderA = [(c, True) for c in range(0, 19)]
    orderB = [(1, True), (0, True)] + [(2 + c, False) for c in range(31, 16, -1)] + [(2 + c, True) for c in range(16, -1, -1)]

    def phase_ret():
        ph = Phase()
        dmask = ph.sb("dmask", [128, 2, 4, 128], F32)
        qdec = ph.sb("qdec", [128, 2, 4, 128], F32)
        kdec = ph.sb("kdec", [128, 2, 4], F32)
        cdec = ph.sb("cdec", [128, 2, 4], F32)
        P.dma("sp", dmask, dmask[:], ret_dmask, ret_dmask[:])
        P.dma("act", qdec, qdec[:], ret_qdec, ret_qdec[:].partition_broadcast(128))
        P.dma("sp", kdec, kdec[:], ret_kdec, ret_kdec[:])
        P.dma("act", cdec, cdec[:], ret_cdec, ret_cdec[:])
        R = ph.sb("R", [128, 4, 2, 256], F32)
        Rb = ph.sb("Rb", [128, 4, 2, 256], BF16)
        Rh = [Buf("R%d" % h) for h in range(4)]
        nb = 2
        qTs = [ph.sb("qT%d" % i, [128, 8, 128], BF16) for i in range(nb)]
        kTs = [ph.sb("kT%d" % i, [128, 8, 128], BF16) for i in range(nb)]
        kts = [ph.sb("ktm%d" % i, [128, 4, 256], BF16) for i in range(nb)]
        vts = [ph.sb("vtm%d" % i, [128, 4, 256], BF16) for i in range(nb)]
        oas = [ph.sb("oa%d" % i, [128, 8, 128], F32) for i in range(nb)]
        osb = [ph.sb("osb%d" % i, [128, 8, 128], F32) for i in range(nb)]
        ks = ph.sb("ks", [128, 4, 256], BF16)
        qs = ph.sb("qs", [128, 8, 128], BF16)
        sTb = ph.sb("sTb", [128, 4, 128], BF16)
        it = 0
        for X, order in ((0, orderA), (1, orderB)):
            if X == 1:
                C.barrier()
            C.op("dve", [], [R] + Rh, lambda: nc.vector.memset(R[:], 0.0))
            P.memset("dve", Rb, Rb[:], 0.0)
            for (cid, full) in order:
                bi = it % nb
                it += 1
                t0 = cid * 128
                qT, kT, kt, vt, oa, ob = qTs[bi], kTs[bi], kts[bi], vts[bi], oas[bi], osb[bi]
                P.dma(P.ldq(), kt, kt[:], Ktm, Ktm[t0:t0 + 128, :].rearrange("t (h d) -> t h d", h=4))
                P.dma(P.ldq(), vt, vt[:], Vtm, Vtm[t0:t0 + 128, :].rearrange("t (h d) -> t h d", h=4))
                if full:
                    P.dma(P.ldq(), qT, qT[:], QT, QT[:, t0:t0 + 128].rearrange("(j p) t -> p j t", p=128))
                    P.dma(P.ldq(), kT, kT[:], KT, KT[:, t0:t0 + 128].rearrange("(j p) t -> p j t", p=128))
                    if X == 1:
                        P.dma(P.ldq(), oa, oa[:], OA, OA[:, t0:t0 + 128].rearrange("(j p) t -> p j t", p=128))
                P.tt("pool", ks, ks[:], kt, kt[:], kdec, kdec[:, X, :].unsqueeze(2).broadcast_to([128, 4, 256]), ALU.mult)
                if full:
                    P.tt("pool", qs, qs[:].rearrange("p (h c) t -> p h c t", h=4), qT, qT[:].rearrange("p (h c) t -> p h c t", h=4),
                         qdec, qdec[:, X, :, :].unsqueeze(2).broadcast_to([128, 4, 2, 128]), ALU.mult)
                    sp = pb[0]

                    def emit_s(kT=kT, qT=qT, sp=sp):
                        last = None
                        for h in range(4):
                            for dc in range(2):
                                last = nc.tensor.matmul(sp[:, h * 128:(h + 1) * 128], lhsT=kT[:, 2 * h + dc, :], rhs=qT[:, 2 * h + dc, :],
                                                        start=(dc == 0), stop=(dc == 1))
                        return last
                    P.mm(sp, [kT, qT], emit_s)
                    P.tt("dve", sTb, sTb[:], sp, sp[:, :].rearrange("p (h n) -> p h n", h=4), dmask, dmask[:, X, :, :], ALU.mult)
                    for hp in range(2):
                        op_ = pb[1 + hp]

                        def emit_o(hp=hp, op_=op_, vt=vt):
                            last = None
                            for hh in range(2):
                                h = 2 * hp + hh
                                for ec in range(2):
                                    o_ap = op_[:, (hh * 2 + ec) * 128:(hh * 2 + ec + 1) * 128]
                                    nc.tensor.matmul(o_ap, lhsT=vt[:, h, ec * 128:(ec + 1) * 128], rhs=sTb[:, h, :], start=True, stop=False)
                                    for dc in range(2):
                                        last = nc.tensor.matmul(o_ap, lhsT=Rb[:, h, dc, ec * 128:(ec + 1) * 128], rhs=qs[:, 2 * h + dc, :],
                                                                start=False, stop=(dc == 1))
                            return last
                        P.mm(op_, [vt, sTb, Rb, qs], emit_o)
                        o_dst = ob[:, hp * 4:(hp + 1) * 4, :]
                        o_src = op_[:, :].rearrange("p (j n) -> p j n", j=4)
                        if X == 0:
                            P.copy("act", ob, o_dst, op_, o_src)
                        else:
                            P.tt("dve", ob, o_dst, op_, o_src, oa, oa[:, hp * 4:(hp + 1) * 4, :], ALU.add)
                    dst = OA if X == 0 else OT
                    P.dma(P.ldq(), dst, dst[:, t0:t0 + 128].rearrange("(j p) t -> p j t", p=128), ob, ob[:])
                for h in range(4):
                    rp = pb[3 + h]

                    def emit_r(h=h, rp=rp, vt=vt):
                        last = None
                        for dc in range(2):
                            last = nc.tensor.matmul(rp[:, dc * 256:(dc + 1) * 256], lhsT=ks[:, h, dc * 128:(dc + 1) * 128], rhs=vt[:, h, :],
                                                    start=True, stop=True)
                        return last
                    P.mm(rp, [ks, vt, Rb], emit_r)
                    P.stt(Rh[h], R[:, h, :, :], Rh[h], R[:, h, :, :], cdec[:, X, h:h + 1], rp, rp[:, :].rearrange("p (c e) -> p c e", c=2),
                          ALU.mult, ALU.add, extra_r=[cdec])
                P.C.op("act", Rh, [Rb], lambda: nc.scalar.copy(out=Rb[:], in_=R[:]))
        ph.close()

    w_out0 = P.din("even_w_out", [D, D])
    glu_w = P.din("glu_w", [1024, 1024])
    mlp_w1 = P.din("mlp_w1", [2, D, 4 * D])
    mlp_w2 = P.din("mlp_w2", [2, 4 * D, D])
    w_in1 = P.din("odd_w_in_ext", [D, 4096])
    w_out1 = P.din("odd_w_out", [D, D])
    if not os.environ.get('K_SKIP_RET'):
        phase_ret()
    if stop_after == "ret":
        return finish(P)

    s5_lre = P.din("s5_lre", [128, 64])
    s5_lim = P.din("s5_lim", [128, 64])
    s5_ldt = P.din("s5_ldt", [128, 64])
    s5_B = P.din("s5_B", [128, 64, 2, 32])
    s5_C = P.din("s5_C", [128, 64, 2, 32])
    YA = P.dscr("YA", [1024, NFULL], F32)
    YB = P.dscr("YB", [1024, NFULL], F32)
    TWO_PI = 2.0 * np.pi
    C1 = float(np.float32(TWO_PI))
    C2 = float(TWO_PI - np.float64(np.float32(TWO_PI)))
    MAGIC = 12582912.0

    def phase_s5():
        ph = Phase()
        SS = 4
        WinR = ph.sb("WinR", [128, 64, 2, 128], BF16)
        CAt = ph.sb("CAt", [128, 64, 4, 2, 32], BF16)
        Ktb = ph.sb("Ktb", [128, 64, 4, 32], BF16)
        AR2 = ph.sb("AR2", [128, 2, 64], F32)
        AI2 = ph.sb("AI2", [128, 2, 64], F32)
        zeros_f = ph.sb("zeros_f", [128, 32], F32)
        P.memset("dve", zeros_f, zeros_f[:], 0.0)
        ph2 = Phase()
        tn = ["lr", "li", "dt", "zr", "zi", "mag", "sn", "cs", "den", "nr", "fr", "fi", "t1", "t2", "kk"]
        tb = {n: ph2.sb("s5t_" + n, [128, 64], F32) for n in tn}
        pw = [(ph2.sb("s5pr%d" % m, [128, 64], F32), ph2.sb("s5pi%d" % m, [128, 64], F32)) for m in range(5)]
        P.dma("sp", tb["lr"], tb["lr"][:], s5_lre, s5_lre[:])
        P.dma("act", tb["li"], tb["li"][:], s5_lim, s5_lim[:])
        P.dma("sp", tb["dt"], tb["dt"][:], s5_ldt, s5_ldt[:])
        V = lambda n: tb[n][:]
        def TS(o, a, s1, s2, op0, op1=None): P.ts("dve", tb[o], V(o), tb[a], V(a), s1, s2, op0, op1)
        def TT(o, a, b, op): P.tt("dve", tb[o], V(o), tb[a], V(a), tb[b], V(b), op)
        def STT(o, a, sc, b, op0, op1): P.stt(tb[o], V(o), tb[a], V(a), sc, tb[b], V(b), op0, op1)
        TS("lr", "lr", -1e-4, None, ALU.min)
        P.act(tb["dt"], V("dt"), tb["dt"], V("dt"), AF.Exp)
        TT("zr", "lr", "dt", ALU.mult)
        TT("zi", "li", "dt", ALU.mult)
        P.act(tb["mag"], V("mag"), tb["zr"], V("zr"), AF.Exp)

        def sin_of(dst, src, shift):
            TS("t1", src, shift, None, ALU.add)
            TS("kk", "t1", 1.0 / TWO_PI, None, ALU.mult)
            TS("kk", "kk", MAGIC, None, ALU.add)
            TS("kk", "kk", -MAGIC, None, ALU.add)
            STT("t1", "kk", -C1, "t1", ALU.mult, ALU.add)
            STT("t1", "kk", -C2, "t1", ALU.mult, ALU.add)
            TS("t1", "t1", 3.1415925, -3.1415925, ALU.min, ALU.max)
            P.act(tb[dst], V(dst), tb["t1"], V("t1"), AF.Sin)
        sin_of("sn", "zi", 0.0)
        sin_of("cs", "zi", float(np.pi / 2))
        ar, ai = pw[1]
        P.tt("dve", ar, ar[:], tb["mag"], V("mag"), tb["cs"], V("cs"), ALU.mult)
        P.tt("dve", ai, ai[:], tb["mag"], V("mag"), tb["sn"], V("sn"), ALU.mult)
        TT("den", "lr", "lr", ALU.mult)
        TT("t1", "li", "li", ALU.mult)
        TT("den", "den", "t1", ALU.add)
        C.op("dve", [tb["den"]], [tb["den"]], lambda: nc.vector.reciprocal(out=V("den"), in_=V("den")))
        P.ts("dve", tb["nr"], V("nr"), ar, ar[:], -1.0, None, ALU.add)
        TT("t1", "nr", "lr", ALU.mult)
        P.tt("dve", tb["t2"], V("t2"), ai, ai[:], tb["li"], V("li"), ALU.mult)
        TT("fr", "t1", "t2", ALU.add)
        TT("fr", "fr", "den", ALU.mult)
        P.tt("dve", tb["t1"], V("t1"), ai, ai[:], tb["lr"], V("lr"), ALU.mult)
        TT("t2", "nr", "li", ALU.mult)
        TT("fi", "t1", "t2", ALU.subtract)
        TT("fi", "fi", "den", ALU.mult)
        P.memset("dve", pw[0][0], pw[0][0][:], 1.0)
        P.memset("dve", pw[0][1], pw[0][1][:], 0.0)
        for m in range(2, 5):
            pr, pi = pw[m]
            qr, qi = pw[m - 1]
            P.tt("dve", pr, pr[:], qr, qr[:], ar, ar[:], ALU.mult)
            P.tt("dve", tb["t1"], V("t1"), qi, qi[:], ai, ai[:], ALU.mult)
            P.tt("dve", pr, pr[:], pr, pr[:], tb["t1"], V("t1"), ALU.subtract)
            P.tt("dve", pi, pi[:], qr, qr[:], ai, ai[:], ALU.mult)
            P.tt("dve", tb["t1"], V("t1"), qi, qi[:], ar, ar[:], ALU.mult)
            P.tt("dve", pi, pi[:], pi, pi[:], tb["t1"], V("t1"), ALU.add)
        P.copy("dve", AR2, AR2[:, 0, :], pw[4][0], pw[4][0][:])
        P.copy("dve", AR2, AR2[:, 1, :], pw[4][0], pw[4][0][:])
        P.ts("dve", AI2, AI2[:, 0, :], pw[4][1], pw[4][1][:], -1.0, None, ALU.mult)
        P.copy("dve", AI2, AI2[:, 1, :], pw[4][1], pw[4][1][:])
        Bc = ph2.sb("s5Bc", [128, 32, 2, 32], F32)
        Cc = ph2.sb("s5Cc", [128, 32, 2, 32], F32)
        Bb = ph2.sb("s5Bb", [128, 32, 2, 32], F32)
        tA = ph2.sb("s5tA", [128, 32, 32], F32)
        tB = ph2.sb("s5tB", [128, 32, 32], F32)
        WinC = ph2.sb("s5WinC", [128, 32, 2, 4, 32], F32)
        CAf = ph2.sb("s5CAf", [128, 32, 4, 2, 32], F32)
        CA4 = ph2.sb("s5CA4", [128, 32, 2, 32], F32)
        for X in range(2):
            sl = slice(X * 32, X * 32 + 32)
            P.dma("act", Bc, Bc[:], s5_B, s5_B[:, sl, :, :])
            P.dma("sp", Cc, Cc[:], s5_C, s5_C[:, sl, :, :])
            bcx = lambda buf: buf[:, sl].unsqueeze(2).broadcast_to([128, 32, 32])

            def cmul(dst_b, dst_re, dst_im, src_b, s_re, s_im, a_re, a_im, neg_im=False):
                P.tt("dve", dst_b, dst_re, src_b, s_re, a_re, bcx(a_re), ALU.mult)
                P.tt("dve", tA, tA[:], src_b, s_im, a_im, bcx(a_im), ALU.mult)
                P.tt("dve", dst_b, dst_re, dst_b, dst_re, tA, tA[:], ALU.subtract)
                P.tt("dve", dst_b, dst_im, src_b, s_re, a_im, bcx(a_im), ALU.mult)
                P.tt("dve", tB, tB[:], src_b, s_im, a_re, bcx(a_re), ALU.mult)
                if neg_im:
                    P.stt(dst_b, dst_im, dst_b, dst_im, -1.0, tB, tB[:], ALU.mult, ALU.subtract)
                else:
                    P.tt("dve", dst_b, dst_im, dst_b, dst_im, tB, tB[:], ALU.add)
            cmul(Bb, Bb[:, :, 0, :], Bb[:, :, 1, :], Bc, Bc[:, :, 0, :], Bc[:, :, 1, :], tb["fr"], tb["fi"])
            for s4 in range(4):
                e = (3 - s4) if X == 0 else s4
                cmul(WinC, WinC[:, :, 0, s4, :], WinC[:, :, 1, s4, :], Bb, Bb[:, :, 0, :], Bb[:, :, 1, :], pw[e][0], pw[e][1])
            for lag in range(4):
                cmul(CAf, CAf[:, :, lag, 0, :], CAf[:, :, lag, 1, :], Cc, Cc[:, :, 0, :], Cc[:, :, 1, :], pw[lag][0], pw[lag][1], neg_im=True)
            cmul(CA4, CA4[:, :, 0, :], CA4[:, :, 1, :], Cc, Cc[:, :, 0, :], Cc[:, :, 1, :], pw[4][0], pw[4][1], neg_im=True)
            for t4 in range(4):
                m = (t4 + 1) if X == 0 else (4 - t4)
                if m == 4:
                    P.copy("act", CAt, CAt[:, sl, t4, :, :], CA4, CA4[:])
                else:
                    P.copy("act", CAt, CAt[:, sl, t4, :, :], CAf, CAf[:, :, m, :, :])
            for g4 in range(16):
                pp = pb[g4 % 4]

                def emit(g4=g4, pp=pp):
                    last = None
                    for jj in range(4):
                        idx = g4 * 4 + jj
                        pl_, ri = idx // 2, idx % 2
                        last = nc.tensor.transpose(out=pp[:, jj * 128:(jj + 1) * 128], in_=WinC[:, pl_, ri, :, :], identity=ident_f[:])
                    return last
                P.mm(pp, [WinC, ident_f], emit)
                pd0 = X * 32 + g4 * 2
                P.copy("act", WinR, WinR[:, pd0:pd0 + 2, :, :], pp, pp[:, :].rearrange("p (a r n) -> p a r n", a=2, r=2))
            for g4 in range(8):
                pp = pb[4 + g4 % 2]

                def emit(g4=g4, pp=pp, X=X):
                    last = None
                    for a in range(4):
                        pl_ = g4 * 4 + a
                        for t4 in range(4):
                            for s4 in range(4):
                                lag = (t4 - s4) if X == 0 else (s4 - t4)
                                o_ap = pp[32 * s4:32 * s4 + 32, (a * 4 + t4) * 32:(a * 4 + t4 + 1) * 32]
                                kw = {"tile_position": (0, 96)} if s4 == 3 else {}
                                if lag < 0:
                                    last = nc.tensor.matmul(o_ap, lhsT=Bb[:, pl_, 0, :], rhs=zeros_f[:, :], start=True, stop=True, **kw)
                                else:
                                    nc.tensor.matmul(o_ap, lhsT=Bb[:, pl_, 0, :], rhs=CAf[:, pl_, lag, 0, :], start=True, stop=False, **kw)
                                    last = nc.tensor.matmul(o_ap, lhsT=Bb[:, pl_, 1, :], rhs=CAf[:, pl_, lag, 1, :], start=False, stop=True, **kw)
                    return last
                P.mm(pp, [Bb, CAf, zeros_f], emit)
                pd0 = X * 32 + g4 * 4
                P.copy("act", Ktb, Ktb[:, pd0:pd0 + 4, :, :], pp, pp[:, :].rearrange("p (a t q) -> p a t q", a=4, t=4))
        ph2.close()

        J = 32
        Hhs = [ph.sb("Hh%d" % i, [128, J, 2, 64], F32) for i in range(2)]
        Hb = ph.sb("Hb", [128, J, 2, 64], BF16)
        Hc = ph.sb("Hc", [128, 2, 64], F32)
        Xbs = [ph.sb("Xb%d" % i, [128, J, 2, 64], F32) for i in range(2)]
        Ups = [[ph.sb("Up%d_%d" % (X, i), [128, 32, J], BF16) for i in range(2)] for X in range(2)]
        tmp1 = ph.sb("tmp1", [128, 2, 64], F32)
        tmp2 = ph.sb("tmp2", [128, 2, 64], F32)
        ysb = [[ph.sb("ysb%d_%d" % (X, i), [128, 8, 128], F32) for i in range(2)] for X in range(2)]
        P.memset("dve", Hc, Hc[:], 0.0)
        nsteps = len(orderB)

        def s5_setup(j):
            doA = j < len(orderA)
            dirs = [0, 1] if doA else [1]
            lo, hi = (0, 64) if doA else (32, 64)
            info = {X: (orderA[j] if X == 0 else orderB[j]) for X in dirs}
            Xb, Hh = Xbs[j % 2], Hhs[j % 2]
            return doA, dirs, lo, hi, info, Xb, Hh

        def s5_loads(j):
            doA, dirs, lo, hi, info, Xb, Hh = s5_setup(j)
            for X in dirs:
                cid = info[X][0]
                Up = Ups[X][j % 2]
                for s4 in range(4):
                    P.dma(P.ldq(), Up, Up[32 * s4:32 * s4 + 32, :, :], UT4, UT4[s4, :, cid * J:(cid + 1) * J].rearrange("(pr p) j -> p pr j", p=32))

        def s5_x(j):
            doA, dirs, lo, hi, info, Xb, Hh = s5_setup(j)
            for X in dirs:
                Up = Ups[X][j % 2]
                for g4 in range(4):
                    pp = pb[g4 % 4]

                    def emit(g4=g4, pp=pp, Up=Up, X=X):
                        last = None
                        for jj in range(16):
                            pair, ri = g4 * 8 + jj // 2, jj % 2
                            last = nc.tensor.matmul(pp[:, jj * J:(jj + 1) * J], lhsT=WinR[:, X * 32 + pair, ri, :], rhs=Up[:, pair, :],
                                                    start=True, stop=True)
                        return last
                    P.mm(pp, [WinR, Up], emit)
                    pd0 = X * 32 + g4 * 8
                    src = pp[:, :].rearrange("p (a r t) -> p a r t", a=8, r=2)
                    if X == 0:
                        dst = Xb[:, :, :, pd0:pd0 + 8].rearrange("p t r a -> p a r t")
                    else:
                        dst = Xb[:, ::-1, :, pd0:pd0 + 8].rearrange("p t r a -> p a r t")
                    P.copy("act", Xb, dst, pp, src)

        s5_loads(0)
        s5_x(0)
        for j in range(nsteps):
            doA, dirs, lo, hi, info, Xb, Hh = s5_setup(j)
            if j + 1 < nsteps:
                s5_loads(j + 1)
            for _ in range(3):
                precast_step()
            need_out = [X for X in dirs if info[X][1]]
            if need_out:
                for X in need_out:
                    sl = slice(X * 32, X * 32 + 32)
                    P.copy("act", Hb, Hb[:, 0 if X == 0 else J - 1, :, sl], Hc, Hc[:, :, sl])
            if j + 1 < nsteps:
                s5_x(j + 1)
            for i in range(J):
                pbuf = Hc if i == 0 else Hh
                prev = Hc[:, :, lo:hi] if i == 0 else Hh[:, i - 1, :, lo:hi]
                prev_sw = Hc[:, ::-1, lo:hi] if i == 0 else Hh[:, i - 1, ::-1, lo:hi]
                P.tt("dve", tmp1, tmp1[:, :, lo:hi], pbuf, prev, AR2, AR2[:, :, lo:hi], ALU.mult)
                P.tt("dve", tmp2, tmp2[:, :, lo:hi], pbuf, prev_sw, AI2, AI2[:, :, lo:hi], ALU.mult)
                P.tt("dve", tmp1, tmp1[:, :, lo:hi], tmp1, tmp1[:, :, lo:hi], tmp2, tmp2[:, :, lo:hi], ALU.add)
                P.tt("dve", Hh, Hh[:, i, :, lo:hi], tmp1, tmp1[:, :, lo:hi], Xb, Xb[:, i, :, lo:hi], ALU.add)
            P.copy("dve", Hc, Hc[:, :, lo:hi], Hh, Hh[:, J - 1, :, lo:hi])
            for X in need_out:
                sl = slice(X * 32, X * 32 + 32)
                if X == 0:
                    P.copy("act", Hb, Hb[:, 1:J, :, sl], Hh, Hh[:, 0:J - 1, :, sl])
                else:
                    P.copy("act", Hb, Hb[:, 0:J - 1, :, sl], Hh, Hh[:, J - 2::-1, :, sl] if False else Hh[:, 0:J - 1, :, sl][:, ::-1, :, :])
                Up = Ups[X][j % 2]
                yb = ysb[X][j % 2]
                for hf in range(2):
                    pp = pb[4 + hf + 2 * X]

                    def emit(pp=pp, X=X, hf=hf, Up=Up):
                        last = None
                        for k4 in range(4):
                            kt = hf * 4 + k4
                            for pl in range(4):
                                pair = kt * 4 + pl
                                pd = X * 32 + pair
                                kw = {"tile_position": (0, 96)} if pl == 3 else {}
                                for t4 in range(4):
                                    o_ap = pp[32 * pl:32 * pl + 32, (k4 * 4 + t4) * J:(k4 * 4 + t4 + 1) * J]
                                    nc.tensor.matmul(o_ap, lhsT=Ktb[:, pd, t4, :], rhs=Up[:, pair, :], start=True, stop=False, **kw)
                                    nc.tensor.matmul(o_ap, lhsT=CAt[:, pd, t4, 0, :], rhs=Hb[:, :, 0, pd], start=False, stop=False, **kw)
                                    last = nc.tensor.matmul(o_ap, lhsT=CAt[:, pd, t4, 1, :], rhs=Hb[:, :, 1, pd], start=False, stop=True, **kw)
                        return last
                    P.mm(pp, [Ktb, CAt, Hb, Up], emit)
                    P.copy("act", yb, yb[:, hf * 4:hf * 4 + 4, :].rearrange("p k (j s) -> p k s j", s=4), pp, pp[:, :].rearrange("p (k s j) -> p k s j", k=4, s=4))
                cid = info[X][0]
                dst = YA if X == 0 else YB
                P.dma(P.ldq(), dst, dst[:, cid * 128:(cid + 1) * 128].rearrange("(k p) t -> p k t", p=128), yb, yb[:])
        while precast_step():
            pass
        ph.close()

    if not os.environ.get('K_NO_PRECAST'):
        w_out0 = precast(w_out0, "w_out0_b")
        glu_w = precast(glu_w, "glu_w_b")
        mlp_w1 = precast(mlp_w1, "mlp_w1_b")
        mlp_w2 = precast(mlp_w2, "mlp_w2_b")
        w_in1 = precast(w_in1, "w_in1_b")
        w_out1 = precast(w_out1, "w_out1_b")
    if not os.environ.get('K_SKIP_S5'):
        phase_s5()
    if stop_after == "s5":
        return finish(P)

    glub_col = P.din("glub_col", [128, 8])
    dskip_col = P.din("dskip_col", [128, 8])
    X1a = P.dscr("X1a", [NFULL, D], F32)
    X1 = P.dscr("X1", [NFULL, D], F32)
    ones_b = P.sb("ones_b", [128, 128], BF16)
    P.memset("dve", ones_b, ones_b[:], 1.0)

    def mlp_tile(ph_bufs, xbuf, nch, layer, src, x1_dram_rows, out_dram, out_rows, GG, ggsrc, xk=None):
        T = nch * 128
        aT, BaT, tb, x1rs, ssb, junk = ph_bufs["aT"], ph_bufs["BaT"], ph_bufs["tb"], ph_bufs["x1r"], ph_bufs["ss"], ph_bufs["junk"]
        hT = ph_bufs["hT"]
        xk = xk if xk is not None else [xbuf] * 4
        xb_, xr0 = x1_dram_rows
        hTk = norm_to_hT(ph_bufs, xbuf, nch, layer, src, 1, inplace=True, xk=xk)
        P.dma("sp", GG, GG[:], GGS, ggsrc.partition_broadcast(128))
        k = 0
        for cb in range(16):
            wb = wload(mlp_w1[layer, :, cb * 512:(cb + 1) * 512].rearrange("(kc p) n -> p kc n", p=128))
            for m in range(4):
                pp = pb[k % 6]
                t_ = tb[k % len(tb)]
                k += 1

                def emit(wb=wb, m=m, pp=pp):
                    last = None
                    for kc in range(16):
                        last = nc.tensor.matmul(pp[:, 0:T], lhsT=wb[:, kc, m * 128:(m + 1) * 128], rhs=hT[:, kc, 0:T],
                                                start=(kc == 0), stop=(kc == 15))
                    return last
                P.mm(pp, [wb] + hTk, emit)
                P.act(t_, t_[:, 0:T], pp, pp[:, 0:T], AF.Relu)
                mo = cb * 4 + m
                P.tt("pool" if k % 2 else "dve", BaT[mo], aT[:, mo, 0:T], t_, t_[:, 0:T], t_, t_[:, 0:T], ALU.mult)
        for c in range(min(nch, len(x1rs))):
            P.dma("act", x1rs[c], x1rs[c][:], xb_, xb_[xr0 + c * 128:xr0 + (c + 1) * 128, :])
        ev = 0
        for cb in range(4):
            for ks in range(4):
                wb = wload(mlp_w2[layer, ks * 2048:(ks + 1) * 2048, cb * 512:(cb + 1) * 512].rearrange("(kc p) n -> p kc n", p=128))
                for c in range(nch):
                    pp = pb[c]

                    def emit(wb=wb, c=c, pp=pp, ks=ks):
                        last = None
                        for kc in range(16):
                            last = nc.tensor.matmul(pp[:, :], lhsT=aT[:, ks * 16 + kc, c * 128:(c + 1) * 128], rhs=wb[:, kc, :],
                                                    start=(ks == 0 and kc == 0), stop=(ks == 3 and kc == 15))
                        return last
                    P.mm(pp, [wb] + BaT[ks * 16:(ks + 1) * 16], emit)
            for c in range(nch):
                P.copy("act" if ev % 2 else "dve", xk[c], xbuf[:, c, cb * 512:(cb + 1) * 512], pb[c], pb[c][:, :])
                ev += 1
        row_rstd(junk, ssb, list(xk), lambda c: xbuf[:, c, :], nch, D)
        for c in range(nch):
            x1r = x1rs[c % len(x1rs)]
            if c >= len(x1rs):
                P.dma("act", x1r, x1r[:], xb_, xb_[xr0 + c * 128:xr0 + (c + 1) * 128, :])
            P.stt(xk[c], xbuf[:, c, :], xk[c], xbuf[:, c, :], ssb[:, c:c + 1], GG, GG[:], ALU.mult, ALU.mult, extra_r=[ssb])
            P.tt("dve", xk[c], xbuf[:, c, :], xk[c], xbuf[:, c, :], x1r, x1r[:], ALU.add)
            P.dma("sp", out_dram, out_dram[out_rows + c * 128:out_rows + (c + 1) * 128, :], xk[c], xbuf[:, c, :])

    def phase3():
        ph = Phase()
        ph.wpool(3)
        xbuf = ph.sb("xbuf", [128, 4, D], F32)
        hT = ph.sb("hTcat", [128, 16, 512], BF16)
        hTk = [Buf("hTk%d" % k) for k in range(16)]
        big = ph.sb("big", [128, 32768], BF16)
        aT = big[:].rearrange("p (k t) -> p k t", k=64)
        BaT = [Buf("aT%d" % k) for k in range(64)]

        def fview(i):
            return big[:, i * 8192:(i + 1) * 8192].bitcast(F32).rearrange("p (k t) -> p k t", k=8)
        yv, uv, gv, ov = fview(0), fview(1), fview(2), fview(3)
        By, Bu_, Bg, Bo = Buf("By"), Buf("Bu"), Buf("Bg"), Buf("Bo")
        rv = big[:, 0:16384].bitcast(F32).rearrange("p (c d) -> p c d", c=4)
        y2b = ph.sb("y2b", [128, 8, 512], BF16)
        x1r = ph.sb("x1r", [128, D], F32)
        GG = ph.sb("GG", [128, D], F32)
        tb = [ph.sb("tb%d" % i, [128, 512], F32) for i in range(4)]
        bufs = {"junk": ph.sb("junk", [128, D], BF16), "ss": ph.sb("ss", [128, 4], F32), "hT": hT, "hTk": hTk,
                "aT": aT, "BaT": BaT, "tb": tb, "x1r": [x1r]}
        junk, ssb = bufs["junk"], bufs["ss"]
        glub = ph.sb("glub", [128, 8], F32)
        dsk = ph.sb("dsk", [128, 8], F32)
        P.dma("sp", glub, glub[:], glub_col, glub_col[:])
        P.dma("act", dsk, dsk[:], dskip_col, dskip_col[:])
        for ti, (srcname, r0, nch, full, tt0) in enumerate(tiles[:int(os.environ.get('K_P3_TILES', '99'))]):
            if not full:
                continue
            T = nch * 128
            src = 1 if srcname == "ctx" else 0
            xin = ctx_in if srcname == "ctx" else x_in
            fm = lambda dr: dr[:, tt0:tt0 + T].rearrange("(k p) t -> p k t", p=128)
            P.dma(P.ldq(), xbuf, xbuf[:, 0:nch, :], xin, xin[r0:r0 + T, :].rearrange("(c p) d -> p c d", p=128))
            P.dma(P.ldq(), By, yv[:, :, 0:T], YA, fm(YA))
            P.dma(P.ldq(), Bg, gv[:, :, 0:T], YB, fm(YB))
            P.dma(P.ldq(), Bu_, uv[:, :, 0:T], UTf, fm(UTf))
            P.dma(P.ldq(), Bo, ov[:, :, 0:T], OT, fm(OT))
            P.dma("sp", GG, GG[:], GGS, GGS[0, 0, src].partition_broadcast(128))
            P.tt("dve", By, yv[:, :, 0:T], By, yv[:, :, 0:T], Bg, gv[:, :, 0:T], ALU.add)
            for kt in range(8):
                P.stt(By, yv[:, kt, 0:T], Bu_, uv[:, kt, 0:T], dsk[:, kt:kt + 1], By, yv[:, kt, 0:T], ALU.mult, ALU.add, extra_r=[dsk])
            P.dma(P.ldq(), Bg, gv[:, :, 0:T], GT, fm(GT))
            P.tt("pool", Bu_, uv[:, :, 0:T], By, yv[:, :, 0:T], By, yv[:, :, 0:T], ALU.mult)
            P.ts("pool", Bu_, uv[:, :, 0:T], Bu_, uv[:, :, 0:T], 0.044715, 1.0, ALU.mult, ALU.add)
            P.tt("pool", Bu_, uv[:, :, 0:T], Bu_, uv[:, :, 0:T], By, yv[:, :, 0:T], ALU.mult)
            P.act(Bu_, uv[:, :, 0:T], Bu_, uv[:, :, 0:T], AF.Sigmoid, scale=1.5957691216057308)
            P.tt("dve", By, yv[:, :, 0:T], By, yv[:, :, 0:T], Bu_, uv[:, :, 0:T], ALU.mult)
            P.copy("pool", y2b, y2b[:, :, 0:T], By, yv[:, :, 0:T])
            k = 0
            for cb in range(2):
                wb = wload(glu_w[:, cb * 512:(cb + 1) * 512].rearrange("(kc p) n -> p kc n", p=128))
                for m in range(4):
                    pp = pb[k % 4]
                    k += 1
                    mo = cb * 4 + m

                    def emit(wb=wb, m=m, pp=pp):
                        last = None
                        for kc in range(8):
                            last = nc.tensor.matmul(pp[:, 0:T], lhsT=wb[:, kc, m * 128:(m + 1) * 128], rhs=y2b[:, kc, 0:T],
                                                    start=(kc == 0), stop=(kc == 7))
                        return last
                    P.mm(pp, [wb, y2b], emit)
                    P.act(Bu_, uv[:, mo, 0:T], pp, pp[:, 0:T], AF.Sigmoid, bias=glub[:, mo:mo + 1], extra_r=[glub])
                    P.tt("dve", hT, hT[:, mo, 0:T], By, yv[:, mo, 0:T], Bu_, uv[:, mo, 0:T], ALU.mult)
            P.act(y2b, y2b[:, :, 0:T], Bo, ov[:, :, 0:T], AF.Square)
            for h in range(4):
                pp = pb[4 + (h % 2)]

                def emit(h=h, pp=pp):
                    last = None
                    for ec in range(2):
                        last = nc.tensor.matmul(pp[:, 0:T], lhsT=ones_b[:, :], rhs=y2b[:, 2 * h + ec, 0:T], start=(ec == 0), stop=(ec == 1))
                    return last
                P.mm(pp, [ones_b, y2b], emit)
                P.ts("dve", Bu_, uv[:, h, 0:T], pp, pp[:, 0:T], 1.0 / 256, EPS, ALU.mult, ALU.add)
            P.act(Bu_, uv[:, 0:4, 0:T], Bu_, uv[:, 0:4, 0:T], AF.Sqrt)
            C.op("dve", [Bu_], [Bu_], lambda: nc.vector.reciprocal(out=uv[:, 0:4, 0:T], in_=uv[:, 0:4, 0:T]))
            P.tt("dve", Bo, ov[:, :, 0:T].rearrange("p (h c) t -> p h c t", h=4), Bo, ov[:, :, 0:T].rearrange("p (h c) t -> p h c t", h=4),
                 Bu_, uv[:, 0:4, 0:T].unsqueeze(2).broadcast_to([128, 4, 2, T]), ALU.mult)
            P.act(Bg, gv[:, :, 0:T], Bg, gv[:, :, 0:T], AF.Silu)
            P.tt("pool", hT, hT[:, 8:16, 0:T], Bo, ov[:, :, 0:T], Bg, gv[:, :, 0:T], ALU.mult)
            ev = 0
            for cb in range(4):
                wb = wload(w_out0[:, cb * 512:(cb + 1) * 512].rearrange("(kc p) n -> p kc n", p=128))
                for c in range(nch):
                    pp = pb[c]

                    def emit(wb=wb, c=c, pp=pp):
                        last = None
                        for kc in range(16):
                            last = nc.tensor.matmul(pp[:, :], lhsT=hT[:, kc, c * 128:(c + 1) * 128], rhs=wb[:, kc, :],
                                                    start=(kc == 0), stop=(kc == 15))
                        return last
                    P.mm(pp, [wb, hT], emit)
                    C.op("act" if ev % 2 else "dve", [pp], [By, Bu_],
                         (lambda c=c, cb=cb, pp=pp: nc.scalar.copy(out=rv[:, c, cb * 512:(cb + 1) * 512], in_=pp[:, :])) if ev % 2 else
                         (lambda c=c, cb=cb, pp=pp: nc.vector.tensor_copy(out=rv[:, c, cb * 512:(cb + 1) * 512], in_=pp[:, :])))
                    ev += 1
            row_rstd(junk, ssb, By, lambda c: rv[:, c, :], nch, D)
            for c in range(nch):
                C.op("dve", [By, Bu_, ssb, GG], [By, Bu_], lambda c=c: nc.vector.scalar_tensor_tensor(
                    out=rv[:, c, :], in0=rv[:, c, :], scalar=ssb[:, c:c + 1], in1=GG[:], op0=ALU.mult, op1=ALU.mult))
                C.op("dve", [By, Bu_, xbuf], [xbuf], lambda c=c: nc.vector.tensor_tensor(out=xbuf[:, c, :], in0=xbuf[:, c, :], in1=rv[:, c, :], op=ALU.add))
            P.dma("sp", X1a, X1a[tt0:tt0 + T, :].rearrange("(c p) d -> p c d", p=128), xbuf, xbuf[:, 0:nch, :])
            C.barrier()
            mlp_tile(bufs, xbuf, nch, 0, src, (X1a, tt0), X1, tt0, GG, GGS[0, 1, src])
            C.barrier()
        ph.close()

    if not os.environ.get('K_SKIP_P3'):
        phase3()
    if stop_after == "p3":
        return finish(P)

    sink_col = P.din("sink_col", [128, 16])
    rope_in = P.din("rope_tab", [4, 128, FULL_LAT])
    perm_in = P.din("rope_perm", [128, 128])
    mask_in = P.din("attn_mask", [2, 128, 128])
    Q1T = P.dscr("Q1T", [2048, OWN], BF16)
    K1T = P.dscr("K1T", [1024, NFULL], BF16)
    V1 = P.dscr("V1", [NFULL, 1024], BF16)
    X2a = P.dscr("X2a", [OWN, D], F32)
    out_d = P.dout("out", [OWN, D], F32)

    def phase4a():
        ph = Phase()
        ph.wpool(4)
        xt = ph.sb("xt", [128, 4, D], F32)
        bufs = {"junk": ph.sb("junk", [128, D], BF16), "ss": ph.sb("ss", [128, 4], F32)}
        hT = ph.sb("hT", [128, 16, 512], BF16)
        hTk = [Buf("hTk%d" % k) for k in range(16)]
        bufs["hT"], bufs["hTk"] = hT, hTk
        perm = ph.sb("perm", [128, 128], BF16)
        P.dma("pool", perm, perm[:], perm_in, perm_in[:])
        rt = ph.sb("rt", [128, 4, 512], F32)
        st_b = [ph.sb("stb%d" % i, [128, 4, 512], BF16) for i in range(3)]
        qraw = [ph.sb("qraw%d" % i, [128, 512], BF16) for i in range(2)]
        t1s = [ph.sb("t1_%d" % i, [128, 512], F32) for i in range(2)]
        t2s = [ph.sb("t2_%d" % i, [128, 512], F32) for i in range(2)]
        cnt = {"ps": 0, "sb": 0, "ev": 0, "r": 0}
        for ti, (srcname, r0, nch, full, tt0) in enumerate(tiles):
            if not full:
                continue
            T = nch * 128
            src = 1 if srcname == "ctx" else 0
            lat0 = tt0 - NCTX
            P.dma(P.ldq(), xt, xt[:, 0:nch, :], X1, X1[tt0:tt0 + T, :].rearrange("(c p) d -> p c d", p=128))
            norm_to_hT(bufs, xt, nch, 1, src, 0)
            if src == 0:
                P.dma(P.ldq(), rt, rt[:, :, 0:T], rope_in, rope_in[:, :, lat0:lat0 + T].rearrange("f p t -> p f t"))
            need_q = (src == 0 and lat0 < OWN)
            for cb in range(8):
                kind = "q" if cb < 4 else ("k" if cb < 6 else "v")
                if kind == "q" and not need_q:
                    continue
                wb = wload(w_in1[:, cb * 512:(cb + 1) * 512].rearrange("(kc p) n -> p kc n", p=128))
                sb_ = st_b[cnt["sb"] % 3]
                cnt["sb"] += 1
                if kind in ("q", "k"):
                    for m in range(4):
                        pp = pb[cnt["ps"] % 4]
                        cnt["ps"] += 1

                        def emit(wb=wb, m=m, pp=pp):
                            last = None
                            for kc in range(16):
                                last = nc.tensor.matmul(pp[:, 0:T], lhsT=wb[:, kc, m * 128:(m + 1) * 128], rhs=hT[:, kc, 0:T],
                                                        start=(kc == 0), stop=(kc == 15))
                            return last
                        P.mm(pp, [wb] + hTk, emit)
                        if src == 1:
                            P.copy("act", sb_, sb_[:, m, 0:T], pp, pp[:, 0:T])
                            continue
                        ri = cnt["r"] % 2
                        cnt["r"] += 1
                        qr, t1, t2 = qraw[ri], t1s[ri], t2s[ri]
                        ci, si = (0, 1) if kind == "q" else (2, 3)
                        P.copy("act", qr, qr[:, 0:T], pp, pp[:, 0:T])
                        P.tt("dve", t1, t1[:, 0:T], pp, pp[:, 0:T], rt, rt[:, ci, 0:T], ALU.mult)
                        p2 = pb[4 + (cnt["r"] % 2)]
                        P.mm(p2, [perm, qr], lambda p2=p2, qr=qr: nc.tensor.matmul(p2[:, 0:T], lhsT=perm[:, :], rhs=qr[:, 0:T], start=True, stop=True))
                        P.tt("dve", t2, t2[:, 0:T], p2, p2[:, 0:T], rt, rt[:, si, 0:T], ALU.mult)
                        P.tt("pool", sb_, sb_[:, m, 0:T], t1, t1[:, 0:T], t2, t2[:, 0:T], ALU.add)
                    if kind == "q":
                        rows = slice(cb * 512, (cb + 1) * 512)
                        P.dma(P.ldq(), Q1T, Q1T[rows, lat0:lat0 + T].rearrange("(m p) t -> p m t", p=128), sb_, sb_[:, :, 0:T])
                    else:
                        rows = slice((cb - 4) * 512, (cb - 3) * 512)
                        P.dma(P.ldq(), K1T, K1T[rows, tt0:tt0 + T].rearrange("(m p) t -> p m t", p=128), sb_, sb_[:, :, 0:T])
                else:
                    for c in range(nch):
                        pp = pb[cnt["ps"] % 4]
                        cnt["ps"] += 1

                        def emit(wb=wb, c=c, pp=pp):
                            last = None
                            for kc in range(16):
                                last = nc.tensor.matmul(pp[:, :], lhsT=hT[:, kc, c * 128:(c + 1) * 128], rhs=wb[:, kc, :],
                                                        start=(kc == 0), stop=(kc == 15))
                            return last
                        P.mm(pp, [wb] + hTk, emit)
                        P.copy("act" if cnt["ev"] % 2 else "dve", sb_, sb_[:, c, :], pp, pp[:, :])
                        cnt["ev"] += 1
                    P.dma(P.ldq(), V1, V1[tt0:tt0 + T, (cb - 6) * 512:(cb - 5) * 512].rearrange("(c p) n -> p c n", p=128), sb_, sb_[:, 0:nch, :])
        ph.close()

    phase4a()

    def phase4b():
        ph = Phase()
        wo = ph.sb("wo", [128, 16, D], BF16)
        for cb in range(4):
            P.dma("pool" if w_out1.ap.dtype == F32 else P.ldq(), wo, wo[:, :, cb * 512:(cb + 1) * 512], None,
                  w_out1[:, cb * 512:(cb + 1) * 512].rearrange("(kc p) n -> p kc n", p=128))
        maskb = ph.sb("maskb", [128, 2, 128], BF16)
        P.dma("pool", maskb, maskb[:], mask_in, mask_in[:].rearrange("a k q -> k a q"))
        esink = ph.sb("esink", [128, 16], F32)
        P.dma("sp", esink, esink[:], sink_col, sink_col[:])
        P.act(esink, esink[:], esink, esink[:], AF.Exp)
        Eh = ph.sb("Eh", [128, 2, 128], BF16)
        P.memset("dve", Eh, Eh[:], 0.0)
        P.memset("dve", Eh, Eh[:, 0, 0:64], 1.0)
        P.memset("dve", Eh, Eh[:, 1, 64:128], 1.0)
        Kc = ph.sb("Kc", [128, 8, 256], BF16)
        Vc = ph.sb("Vc", [128, 2, 1024], BF16)
        P.dma("sp", Kc, Kc[:], K1T, K1T[:, 0:256].rearrange("(j p) t -> p j t", p=128))
        P.dma("act", Vc, Vc[:], V1, V1[0:256, :].rearrange("(b p) n -> p b n", p=128))
        GG = ph.sb("GG", [128, D], F32)
        P.dma("sp", GG, GG[:], GGS, GGS[1, 0, 0].partition_broadcast(128))
        Kl = [ph.sb("Kl%d" % i, [128, 8, 384], BF16) for i in range(2)]
        Vl = [ph.sb("Vl%d" % i, [128, 3, 1024], BF16) for i in range(2)]
        qTs = [ph.sb("q1T%d" % i, [128, 16, 128], BF16) for i in range(2)]
        PTs = [ph.sb("PT%d" % i, [128, 512], BF16) for i in range(4)]
        attnT = [ph.sb("attnT%d" % i, [128, 16, 128], BF16) for i in range(2)]
        dtmp = ph.sb("dtmp", [128, 4, 128], F32)
        xc = [ph.sb("xc%d" % i, [128, D], F32) for i in range(2)]
        rsb = ph.sb("rsb", [128, D], F32)
        junk = ph.sb("junk", [128, D], BF16)
        ssb = ph.sb("ss", [128, 4], F32)
        npt = 0
        for c in range(16):
            bi = c % 2
            kl, vl, qT, aT_, xcb = Kl[bi], Vl[bi], qTs[bi], attnT[bi], xc[bi]
            lo_c = max(c - 1, 0)
            nloc = (c + 2 - lo_c)
            r0 = NCTX + lo_c * 128
            P.dma(P.ldq(), kl, kl[:, :, 0:nloc * 128], K1T, K1T[:, r0:r0 + nloc * 128].rearrange("(j p) t -> p j t", p=128))
            P.dma(P.ldq(), vl, vl[:, 0:nloc, :], V1, V1[r0:r0 + nloc * 128, :].rearrange("(b p) n -> p b n", p=128))
            P.dma(P.ldq(), qT, qT[:], Q1T, Q1T[:, c * 128:(c + 1) * 128].rearrange("(j p) t -> p j t", p=128))
            P.dma(P.ldq(), xcb, xcb[:], X1, X1[NCTX + c * 128:NCTX + (c + 1) * 128, :])
            blocks = [("c", 0, None), ("c", 1, None)]
            for lc in range(nloc):
                chunk = lo_c + lc
                mk = 0 if chunk == c - 1 else (1 if chunk == c + 1 else None)
                blocks.append(("l", lc, mk))
            jobs = []
            for kvh in range(4):
                for bi_, (kind, bidx, mk) in enumerate(blocks):
                    for par in range(2):
                        var = kvh * 2 + par
                        if kind == "c":
                            jobs.append((kvh, Kc[:, var, bidx * 128:(bidx + 1) * 128], Vc[:, bidx, var * 128:(var + 1) * 128], Kc, Vc, mk, par,
                                         bi_ == 0 and par == 0, bi_ == len(blocks) - 1 and par == 1))
                        else:
                            jobs.append((kvh, kl[:, var, bidx * 128:(bidx + 1) * 128], vl[:, bidx, var * 128:(var + 1) * 128], kl, vl, mk, par,
                                         bi_ == 0 and par == 0, bi_ == len(blocks) - 1 and par == 1))

            def emit_S(i):
                kvh, k_ap, v_ap, kb_, vb_, mk, par, f, l = jobs[i]
                sp_ = pb[(npt0 + i) % 2]
                q_ap = qT[:, 4 * kvh:4 * kvh + 4, :]

                def emit_s():
                    last = nc.tensor.matmul(sp_[:, :], lhsT=k_ap, rhs=q_ap, start=True, stop=(mk is None))
                    if mk is not None:
                        last = nc.tensor.matmul(sp_[:, :], lhsT=ident_b[:, :], rhs=maskb[:, mk, :].unsqueeze(1).broadcast_to([128, 4, 128]),
                                                start=False, stop=True)
                    return last
                P.mm(sp_, [kb_, qT, ident_b, maskb], emit_s)
            npt0 = npt
            emit_S(0)
            for i in range(len(jobs)):
                kvh, k_ap, v_ap, kb_, vb_, mk, par, f, l = jobs[i]
                if i + 1 < len(jobs):
                    emit_S(i + 1)
                sp_ = pb[(npt0 + i) % 2]
                PT = PTs[(npt0 + i) % 4]
                pv, den = (pb[2], pb[3]) if kvh % 2 == 0 else (pb[4], pb[5])
                P.act(PT, PT[:, :], sp_, sp_[:, :], AF.Exp)
                P.mm(pv, [vb_, PT], lambda v_ap=v_ap, PT=PT, f=f, l=l, pv=pv: nc.tensor.matmul(pv[:, :], lhsT=v_ap, rhs=PT[:, :], start=f, stop=l))
                P.mm(den, [Eh, PT], lambda par=par, PT=PT, f=f, l=l, den=den: nc.tensor.matmul(den[:, :], lhsT=Eh[:, par, :], rhs=PT[:, :], start=f, stop=l))
                if l:
                    P.tt("dve", dtmp, dtmp[:], den, den[:, :].rearrange("p (j n) -> p j n", j=4), esink,
                         esink[:, 4 * kvh:4 * kvh + 4].unsqueeze(2).broadcast_to([128, 4, 128]), ALU.add)
                    C.op("dve", [dtmp], [dtmp], lambda: nc.vector.reciprocal(out=dtmp[:], in_=dtmp[:]))
                    P.tt("dve", aT_, aT_[:, 4 * kvh:4 * kvh + 4, :], pv, pv[:, :].rearrange("p (j n) -> p j n", j=4), dtmp, dtmp[:], ALU.mult)
            npt += len(jobs)
            for cb in range(4):
                pp = pb[6 + cb % 2]

                def emit(cb=cb, pp=pp, aT_=aT_):
                    last = None
                    for kc in range(16):
                        last = nc.tensor.matmul(pp[:, :], lhsT=aT_[:, kc, :], rhs=wo[:, kc, cb * 512:(cb + 1) * 512], start=(kc == 0), stop=(kc == 15))
                    return last
                P.mm(pp, [aT_, wo], emit)
                P.copy("act", rsb, rsb[:, cb * 512:(cb + 1) * 512], pp, pp[:, :])
            row_rstd(junk, ssb, rsb, lambda c_: rsb[:, :], 1, D)
            P.stt(rsb, rsb[:], rsb, rsb[:], ssb[:, 0:1], GG, GG[:], ALU.mult, ALU.mult, extra_r=[ssb])
            P.tt("dve", xcb, xcb[:], xcb, xcb[:], rsb, rsb[:], ALU.add)
            P.dma(P.ldq(), X2a, X2a[c * 128:(c + 1) * 128, :], xcb, xcb[:])
        ph.close()

    phase4b()
    if stop_after == "p4":
        return finish(P)

    def phase5():
        ph = Phase()
        ph.wpool(3)
        xbuf = ph.sb("xbuf", [128, 4, D], F32)
        xk = [Buf("xk%d" % c) for c in range(4)]
        hT = ph.sb("hT5", [128, 16, 512], BF16)
        hTk = [Buf("hTk%d" % k) for k in range(16)]
        big = ph.sb("big5", [128, 32768], BF16)
        aT = big[:].rearrange("p (k t) -> p k t", k=64)
        BaT = [Buf("aT%d" % k) for k in range(64)]
        GG = ph.sb("GG5", [128, D], F32)
        bufs = {"junk": ph.sb("junk", [128, D], BF16), "ss": ph.sb("ss", [128, 4], F32), "hT": hT, "hTk": hTk,
                "aT": aT, "BaT": BaT,
                "tb": [ph.sb("tb%d" % i, [128, 512], F32) for i in range(4)],
                "x1r": [ph.sb("x1r%d" % i, [128, D], F32) for i in range(2)]}
        for t0 in range(0, OWN, 512):
            for c in range(4):
                P.dma(P.ldq(), xk[c], xbuf[:, c, :], X2a, X2a[t0 + c * 128:t0 + (c + 1) * 128, :])
            mlp_tile(bufs, xbuf, 4, 1, 0, (X2a, t0), out_d, t0, GG, GGS[1, 1, 0], xk=xk)
        ph.close()

    phase5()
    return finish(P)


def finish(P):
    P.C.barrier()
    return P


def col16(v):
    return np.ascontiguousarray(np.asarray(v).reshape(16, 128).T)


def host_inputs(core, inp):
    b, s = core // 2, core % 2
    rev = (s == 1)
    x = inp["x"][b]
    ctx = inp["ctx"][b]
    if rev:
        x = x[::-1]
        ctx = ctx[::-1]
    m = {}
    m["x_loc"] = np.ascontiguousarray(x, dtype=np.float32)
    m["ctx_loc"] = np.ascontiguousarray(ctx, dtype=np.float32)
    m["c_col"] = np.ascontiguousarray(np.stack([col16(inp["c"][b]), col16(inp["c_ctx"])], axis=1))
    m["mod_w"] = inp["mod_w"]
    mb = inp["mod_b"].reshape(2, 6, D)
    m["modb_col"] = np.ascontiguousarray(np.stack([np.stack([col16(mb[i, j]) for j in (0, 1, 3, 4)], axis=1) for i in range(2)], axis=1))
    m["modb_row"] = np.ascontiguousarray(mb[:, [2, 5], :])
    ng = inp["norm_g"]
    m["g_col"] = np.ascontiguousarray(np.stack([np.stack([col16(ng[i, j]) for j in (0, 2)], axis=1) for i in range(2)], axis=1))
    m["g_row"] = np.ascontiguousarray(ng[:, [1, 3], :])
    m["ident"] = np.eye(128, dtype=np.float32)
    m["even_w_in"] = inp["even_w_in"][0]
    m["even_w_out"] = inp["even_w_out"][0]
    m["glu_w"] = inp["s5_glu_w"][0]
    m["glub_col"] = np.ascontiguousarray(inp["s5_glu_b"][0].reshape(8, 128).T)
    m["dskip_col"] = np.ascontiguousarray(inp["s5_d"][0].reshape(8, 128).T)
    m["mlp_w1"] = inp["mlp_w1"]
    m["mlp_w2"] = inp["mlp_w2"]
    w1 = inp["odd_w_in"][0]
    ext = np.zeros((D, 4096), np.float32)
    ext[:, :2048] = w1[:, :2048]
    for kv in range(4):
        kcols = w1[:, 2048 + kv * 64: 2048 + (kv + 1) * 64]
        vcols = w1[:, 2304 + kv * 64: 2304 + (kv + 1) * 64]
        for par in range(2):
            base = (kv * 2 + par) * 128 + par * 64
            ext[:, 2048 + base: 2048 + base + 64] = kcols
            ext[:, 3072 + base: 3072 + base + 64] = vcols
    m["odd_w_in_ext"] = ext
    m["odd_w_out"] = inp["odd_w_out"][0]
    sk = inp["odd_sink"][0]
    m["sink_col"] = np.ascontiguousarray(np.stack([np.where(np.arange(128) < 64, sk[2 * c2], sk[2 * c2 + 1]) for c2 in range(16)], axis=1).astype(np.float32))
    tloc = np.arange(FULL_LAT)
    pos = np.where(tloc < L, tloc, 0)
    pos = (L - 1 - pos) if rev else pos
    row = (pos // 64).astype(np.float64); colp = (pos % 64).astype(np.float64)
    inv_freq = 10000.0 ** (-np.arange(16, dtype=np.float64) / 16)
    ang = np.concatenate([row[:, None] * inv_freq[None], colp[:, None] * inv_freq[None]], axis=-1)
    dd = np.arange(128) % 64
    cosT = np.cos(ang[:, dd % 32]).T
    sinT = np.sin(ang[:, dd % 32]).T * np.where(dd < 32, -1.0, 1.0)[:, None]
    m["rope_tab"] = np.stack([cosT * 0.125, sinT * 0.125, cosT, sinT]).astype(np.float32)
    perm = np.zeros((128, 128), np.float32)
    for mm_ in range(128):
        src_ = mm_ + 32 if (mm_ % 64) < 32 else mm_ - 32
        perm[src_, mm_] = 1.0
    m["rope_perm"] = perm
    kk, qq = np.meshgrid(np.arange(128), np.arange(128), indexing="ij")
    m["attn_mask"] = np.stack([np.where(kk >= qq, 0.0, -30000.0), np.where(kk <= qq, 0.0, -30000.0)]).astype(np.float32)
    T = 128
    pos = np.arange(T, dtype=np.float64)
    dmask = np.zeros((128, 2, 4, 128), np.float64)
    qdec = np.zeros((2, 4, 128), np.float64)
    kdec = np.zeros((128, 2, 4), np.float64)
    cdec = np.zeros((128, 2, 4), np.float64)
    for X in range(2):
        d_idx = X if s == 0 else 1 - X
        for h in range(4):
            lg = np.log1p(-np.exp2(-5.0 - (2.0 * h + d_idx)))
            mm, nn = np.meshgrid(pos, pos, indexing="ij")
            if X == 0:
                dmask[:, X, h, :] = np.where(nn >= mm, np.exp((nn - mm) * lg), 0.0) * 256 ** -0.5
                qdec[X, h] = np.exp((pos + 1) * lg)
                kdec[:, X, h] = np.exp((T - 1 - pos) * lg) * 256 ** -0.5
            else:
                dmask[:, X, h, :] = np.where(mm >= nn, np.exp((mm - nn) * lg), 0.0) * 256 ** -0.5
                qdec[X, h] = np.exp((T - pos) * lg)
                kdec[:, X, h] = np.exp(pos * lg) * 256 ** -0.5
            cdec[:, X, h] = np.exp(T * lg)
    lre = np.zeros((128, 64), np.float32); lim = np.zeros((128, 64), np.float32); ldt = np.zeros((128, 64), np.float32)
    Bc = np.zeros((128, 64, 2, 32), np.float32); Cc = np.zeros((128, 64, 2, 32), np.float32)
    for X in range(2):
        d_idx = X if s == 0 else 1 - X
        for g2 in range(2):
            rows = slice(g2 * 64, g2 * 64 + 64)
            cols = slice(X * 32, X * 32 + 32)
            gsel = np.arange(32) * 2 + g2
            lre[rows, cols] = inp["s5_lam_re"][0, d_idx, gsel, :].T
            lim[rows, cols] = inp["s5_lam_im"][0, d_idx, gsel, :].T
            ldt[rows, cols] = inp["s5_log_dt"][0, d_idx, gsel][None, :]
            qs = slice(g2 * 16, g2 * 16 + 16)
            Bc[rows, cols, 0, qs] = inp["s5_b_re"][0, d_idx, gsel].transpose(1, 0, 2)
            Bc[rows, cols, 1, qs] = inp["s5_b_im"][0, d_idx, gsel].transpose(1, 0, 2)
            Cc[rows, cols, 0, qs] = inp["s5_c_re"][0, d_idx, gsel].transpose(2, 0, 1)
            Cc[rows, cols, 1, qs] = inp["s5_c_im"][0, d_idx, gsel].transpose(2, 0, 1)
    m["s5_lre"] = lre; m["s5_lim"] = lim; m["s5_ldt"] = ldt; m["s5_B"] = Bc; m["s5_C"] = Cc
    m["ret_dmask"] = dmask.astype(np.float32)
    m["ret_qdec"] = qdec.astype(np.float32)
    m["ret_kdec"] = kdec.astype(np.float32)
    m["ret_cdec"] = cdec.astype(np.float32)
    return m


_CACHE = {}


def kernel(**inputs):
    inp = {k: np.asarray(v) for k, v in inputs.items()}
    if "prog" not in _CACHE:
        _CACHE["prog"] = build()
    P = _CACHE["prog"]
    maps = []
    for core in range(8):
        hm = host_inputs(core, inp)
        maps.append({k: hm[k] for k in P.inputs})
    res = run_bass_kernel_spmd(P.nc, maps, core_ids=list(range(8)))
    out = np.zeros((4, L, D), np.float32)
    for core in range(8):
        b, s = core // 2, core % 2
        o = np.asarray(res.results[core]["out"])
        if s == 0:
            out[b, :OWN] = o
        else:
            out[b, OWN:] = o[::-1]
    return out
```

```python
import numpy as np
import concourse.bass as bass
import concourse.mybir as mybir
from concourse.bass_utils import run_bass_kernel_spmd

F32 = mybir.dt.float32
BF16 = mybir.dt.bfloat16
AF = mybir.ActivationFunctionType
ALU = mybir.AluOpType

D = 2048
L = 4096
NCTX = 256
OWN = 2048
FULL_LAT = 2176
NFULL = NCTX + FULL_LAT
NALL = NCTX + L
EPS = 1e-6


class Ev:
    __slots__ = ("sem", "val", "closed", "_i", "_q")

    def __init__(self, sem, val):
        self.sem = sem
        self.val = val
        self.closed = True


class Buf:
    def __init__(self, name, ap=None):
        self.name = name
        self.ap = ap
        self.w = None
        self.r = {}
        self.excl = False

    def __getitem__(self, idx):
        return self.ap[idx]


class Group:
    def __init__(self, members, ap=None):
        self.members = list(members)
        self.excl = False
        self.ap = ap

    def __getitem__(self, idx):
        return self.ap[idx]


def _flat(bufs):
    out = []
    for b in bufs:
        if isinstance(b, Group):
            out.extend(b.members)
        else:
            out.append(b)
    return out


class Ctx:
    def __init__(self, nc):
        self.nc = nc
        self.eng = {"pe": nc.tensor, "act": nc.scalar, "dve": nc.vector, "pool": nc.gpsimd, "sp": nc.sync}
        self.sem = {e: nc.alloc_semaphore("s_" + e) for e in self.eng}
        self.cnt = {e: 0 for e in self.eng}
        self.waited = {e: {} for e in self.eng}
        self.dsems = [nc.alloc_semaphore("d%d" % i) for i in range(90)]
        self.dtot = [0] * len(self.dsems)
        self.dnext = 0
        self.semid = {}
        self.all_events = []
        self.n_inst = 0

    def _sid(self, sem):
        return id(sem)

    def need(self, e, ev, strict=False):
        if ev is None:
            return
        if ev.sem is self.sem[e] and not strict and e == "pe":
            return
        ev.closed = True
        k = self._sid(ev.sem)
        if self.waited[e].get(k, 0) < ev.val:
            self.eng[e].wait_ge(ev.sem, ev.val)
            self.waited[e][k] = ev.val

    def deps(self, e, reads, writes, strict=False):
        for b in reads:
            self.need(e, b.w, strict)
        for b in writes:
            self.need(e, b.w, strict)
            for ev in b.r.values():
                self.need(e, ev, strict)

    def record(self, ev, reads, writes):
        for b in writes:
            b.w = ev
            b.r = {}
        for b in reads:
            b.r[self._sid(ev.sem)] = ev

    def op(self, e, reads, writes, emit):
        reads, writes = _flat(reads), _flat(writes)
        writes = list(writes) + [b for b in reads if b.excl]
        reads = [b for b in reads if not b.excl]
        self.deps(e, reads, writes)
        inst = emit()
        self.cnt[e] += 1
        inst.then_inc(self.sem[e], 1)
        ev = Ev(self.sem[e], self.cnt[e])
        self.record(ev, reads, writes)
        self.n_inst += 1
        return ev

    def dma_event(self, q):
        i = self.dnext
        self.dnext = (self.dnext + 1) % len(self.dsems)
        if self.dtot[i] > 0:
            k = self._sid(self.dsems[i])
            if self.waited[q].get(k, 0) < self.dtot[i]:
                self.eng[q].wait_ge(self.dsems[i], self.dtot[i])
                self.waited[q][k] = self.dtot[i]
        ev = Ev(self.dsems[i], self.dtot[i])
        ev.closed = False
        ev._i = i
        ev._q = q
        return ev

    def dma(self, q, out, in_, reads, writes, ev=None, **kw):
        reads, writes = _flat(reads), _flat(writes)
        if ev is None:
            ev = self.dma_event(q)
        assert not ev.closed and ev._q == q
        self.deps(q, reads, writes, strict=True)
        self.eng[q].dma_start(out=out, in_=in_, **kw).then_inc(ev.sem, 16)
        self.dtot[ev._i] += 16
        ev.val = self.dtot[ev._i]
        self.record(ev, reads, writes)
        self.all_events.append(ev)
        return ev

    def barrier(self):
        evs = [Ev(self.sem[e], self.cnt[e]) for e in self.eng if self.cnt[e] > 0]
        evs += [Ev(self.dsems[i], self.dtot[i]) for i in range(len(self.dsems)) if self.dtot[i] > 0]
        for e in self.eng:
            for ev in evs:
                self.need(e, ev)
        for ev in self.all_events:
            ev.closed = True
        self.all_events = []


class Prog:
    def __init__(self, debug=()):
        self.nc = nc = bass.Bass("TRN2", target_bir_lowering=False)
        self.C = Ctx(nc)
        self.debug = set(debug)
        self.inputs = {}
        self.outputs = {}
        self.qrr = 0

    def din(self, name, shape, dt=F32):
        b = Buf(name, self.nc.dram_tensor(name, list(shape), dt, kind="ExternalInput").ap())
        self.inputs[name] = b
        return b

    def dout(self, name, shape, dt=F32):
        b = Buf(name, self.nc.dram_tensor(name, list(shape), dt, kind="ExternalOutput").ap())
        self.outputs[name] = b
        return b

    def dscr(self, name, shape, dt):
        if name in self.debug:
            return self.dout(name, shape, dt)
        return Buf(name, self.nc.dram_tensor(name, list(shape), dt, kind="Internal").ap())

    def sb(self, name, shape, dt):
        return Buf(name, self.nc.alloc_sbuf_tensor(name, list(shape), dt).ap())

    def ps(self, name, shape, dt=F32):
        b = Buf(name, self.nc.alloc_psum_tensor(name, list(shape), dt).ap())
        b.excl = True
        return b

    def act(self, out_b, out, in_b, in_, func, bias=None, scale=None, accum=None, extra_r=(), extra_w=(), e="act"):
        nc = self.nc
        kw = {}
        if bias is not None:
            kw["bias"] = bias
        if scale is not None:
            kw["scale"] = scale
        if accum is not None:
            kw["accum_out"] = accum
        return self.C.op("act", [in_b] + list(extra_r), [out_b] + list(extra_w),
                         lambda: nc.scalar.activation(out=out, in_=in_, func=func, **kw))

    def tt(self, e, out_b, out, a_b, a, b_b, b, op, extra_r=()):
        eng = self.C.eng[e]
        return self.C.op(e, [a_b, b_b] + list(extra_r), [out_b], lambda: eng.tensor_tensor(out=out, in0=a, in1=b, op=op))

    def ts(self, e, out_b, out, a_b, a, s1, s2, op0, op1=None, extra_r=()):
        eng = self.C.eng[e]
        if op1 is None:
            return self.C.op(e, [a_b] + list(extra_r), [out_b],
                             lambda: eng.tensor_scalar(out=out, in0=a, scalar1=s1, scalar2=None, op0=op0))
        return self.C.op(e, [a_b] + list(extra_r), [out_b],
                         lambda: eng.tensor_scalar(out=out, in0=a, scalar1=s1, scalar2=s2, op0=op0, op1=op1))

    def stt(self, out_b, out, a_b, a, scalar, b_b, b, op0, op1, extra_r=()):
        nc = self.nc
        return self.C.op("dve", [a_b, b_b] + list(extra_r), [out_b],
                         lambda: nc.vector.scalar_tensor_tensor(out=out, in0=a, scalar=scalar, in1=b, op0=op0, op1=op1))

    def copy(self, e, out_b, out, in_b, in_):
        eng = self.C.eng[e]
        if e == "act":
            return self.C.op(e, [in_b], [out_b], lambda: eng.copy(out=out, in_=in_))
        return self.C.op(e, [in_b], [out_b], lambda: eng.tensor_copy(out=out, in_=in_))

    def memset(self, e, out_b, out, val):
        eng = self.C.eng[e]
        return self.C.op(e, [], [out_b], lambda: eng.memset(out, val))

    def mm(self, out_b, reads, emit):
        return self.C.op("pe", reads, [out_b], emit)

    def dma(self, q, out_b, out, in_b, in_, ev=None, **kw):
        return self.C.dma(q, out, in_, [in_b] if in_b is not None else [], [out_b], ev=ev, **kw)

    def ldq(self):
        self.qrr ^= 1
        return "sp" if self.qrr else "act"


def build(debug=(), stop_after=None):
    P = Prog(debug)
    nc, C = P.nc, P.C

    x_in = P.din("x_loc", [L, D])
    ctx_in = P.din("ctx_loc", [NCTX, D])
    c_col = P.din("c_col", [128, 2, 16])
    mod_w = P.din("mod_w", [2, D, 6 * D])
    modb_col = P.din("modb_col", [128, 2, 4, 16])
    modb_row = P.din("modb_row", [2, 2, D])
    g_col = P.din("g_col", [128, 2, 2, 16])
    g_row = P.din("g_row", [2, 2, D])
    ident_in = P.din("ident", [128, 128])
    w_in0 = P.din("even_w_in", [D, 5120])

    ident_f = P.sb("ident_f", [128, 128], F32)
    ident_b = P.sb("ident_b", [128, 128], BF16)
    P.dma("sp", ident_f, ident_f[:], ident_in, ident_in[:])
    P.copy("dve", ident_b, ident_b[:], ident_f, ident_f[:])
    colA = P.sb("colA", [128, 2, 2, 2, 16], F32)
    colB = P.sb("colB", [128, 2, 2, 2, 16], F32)
    GGS = P.dscr("GGS", [2, 2, 2, D], F32)
    pb = [P.ps("pb%d" % i, [128, 512], F32) for i in range(8)]
    NW = 4
    wpool = []
    wstate = {"i": 0}

    def wload(view):
        wb = wpool[wstate["i"] % len(wpool)]
        wstate["i"] += 1
        kc, n = view.shape[1], view.shape[2]
        q = "pool" if view.dtype == F32 else P.ldq()
        P.dma(q, wb, wb[:, 0:kc, 0:n], None, view)
        return wb

    def precast(src, name):
        shp = list(src.ap.shape)
        dst = P.dscr(name, shp, BF16)
        s2 = src.ap if len(shp) == 2 else src.ap.rearrange("l k n -> (l k) n")
        d2 = dst.ap if len(shp) == 2 else dst.ap.rearrange("l k n -> (l k) n")
        rows, ncols = s2.shape
        rp = max(128, (1 << 20) // ncols)
        for r0 in range(0, rows, rp):
            r1 = min(rows, r0 + rp)
            precast_q.append((dst, d2[r0:r1, :], s2[r0:r1, :]))
        return dst

    precast_q = []

    def precast_step():
        if not precast_q:
            return False
        dst, o_ap, i_ap = precast_q.pop(0)
        P.dma("pool", dst, o_ap, None, i_ap)
        return True

    class Phase:
        def __init__(self):
            self.cms = []

        def sb(self, name, shape, dt):
            wstate["u"] = wstate.get("u", 0) + 1
            name = "%s_u%d" % (name, wstate["u"])
            cm = nc.sbuf_tensor(name, list(shape), dt)
            t = cm.__enter__()
            self.cms.append(cm)
            return Buf(name, t.ap())

        def wpool(self, n=NW):
            wpool[:] = [self.sb("wt%d" % i, [128, 16, 512], BF16) for i in range(n)]

        def close(self):
            C.barrier()
            for cm in reversed(self.cms):
                cm.__exit__(None, None, None)

    def phase0():
        ph = Phase()
        ph.wpool()
        sc_f = ph.sb("sc_f", [128, 2, 16], F32)
        sc_b = ph.sb("sc_b", [128, 2, 16], BF16)
        mb_col = ph.sb("mb_col", [128, 2, 4, 16], F32)
        gcol = ph.sb("gcol", [128, 2, 2, 16], F32)
        rowb = ph.sb("rowb", [2, 2, 2, D], F32)
        rowg = ph.sb("rowg", [2, 2, 2, D], F32)
        rowo = ph.sb("rowo", [2, D], F32)
        cps = pb[0]
        cpsv = pb[0][:, 0:128].rearrange("p (v m s) -> p v m s", v=4, m=16, s=2)
        rps = [pb[1], pb[2]]
        P.dma("sp", sc_f, sc_f[:], c_col, c_col[:])
        P.dma("act", mb_col, mb_col[:], modb_col, modb_col[:])
        P.dma("sp", gcol, gcol[:], g_col, g_col[:])
        P.dma("act", rowb, rowb[:], modb_row, modb_row[:].partition_broadcast(2))
        P.dma("sp", rowg, rowg[:], g_row, g_row[:].partition_broadcast(2))
        P.act(sc_f, sc_f[:], sc_f, sc_f[:], AF.Silu)
        P.copy("dve", sc_b, sc_b[:], sc_f, sc_f[:])
        JV = {0: 0, 1: 1, 3: 2, 4: 3}
        JG = {2: 0, 5: 1}
        rr = 0
        for i in range(2):
            for j in range(6):
                for cb in range(4):
                    wb = wload(mod_w[i, :, j * D + cb * 512: j * D + (cb + 1) * 512].rearrange("(kc p) n -> p kc n", p=128))
                    if j in JV:
                        v = JV[j]

                        def emit(wb=wb, v=v, cb=cb):
                            last = None
                            for m in range(4):
                                mc = cb * 4 + m
                                for kc in range(16):
                                    last = nc.tensor.matmul(cpsv[:, v, mc, :], lhsT=wb[:, kc, m * 128:(m + 1) * 128],
                                                            rhs=sc_b[:, :, kc], start=(kc == 0), stop=(kc == 15))
                            return last
                        P.mm(cps, [wb, sc_b], emit)
                    else:
                        sub = JG[j]
                        rp = rps[rr % 2]
                        rr += 1

                        def emit(wb=wb, rp=rp):
                            last = None
                            for kc in range(16):
                                last = nc.tensor.matmul(rp[0:2, :], lhsT=sc_b[:, :, kc], rhs=wb[:, kc, :],
                                                        start=(kc == 0), stop=(kc == 15))
                            return last
                        P.mm(rp, [wb, sc_b], emit)
                        sl = slice(cb * 512, (cb + 1) * 512)
                        P.tt("dve", rowo, rowo[:, sl], rp, rp[0:2, :], rowb, rowb[:, i, sub, sl], ALU.add)
                        P.tt("dve", rowo, rowo[:, sl], rowo, rowo[:, sl], rowg, rowg[:, i, sub, sl], ALU.mult)
                        if cb == 3:
                            P.dma("sp", GGS, GGS[i, sub], rowo, rowo[:])
            for s in range(2):
                for sub in range(2):
                    vsh, vsc = 2 * sub, 2 * sub + 1
                    P.tt("dve", colB, colB[:, i, s, sub, :], cps, cpsv[:, vsh, :, s], mb_col, mb_col[:, i, vsh, :], ALU.add)
                    P.tt("dve", colA, colA[:, i, s, sub, :], cps, cpsv[:, vsc, :, s], mb_col, mb_col[:, i, vsc, :], ALU.add)
                    P.stt(colA, colA[:, i, s, sub, :], colA, colA[:, i, s, sub, :], 1.0, gcol, gcol[:, i, sub, :], ALU.add, ALU.mult)
        ph.close()

    import os
    if os.environ.get('K_SKIP_P0'):
        P.memset('dve', colA, colA[:], 1.0)
        P.memset('dve', colB, colB[:], 0.0)
    else:
        phase0()
    if stop_after == "p0":
        dbgA = P.dout("dbg_colA", [128, 2, 2, 2, 16])
        dbgB = P.dout("dbg_colB", [128, 2, 2, 2, 16])
        P.dma("sp", dbgA, dbgA[:], colA, colA[:])
        P.dma("sp", dbgB, dbgB[:], colB, colB[:])
        return finish(P)

    def row_rstd(junk, ssb, src_b, src_ap_fn, nch, width):
        for c in range(nch):
            sb_c = src_b[c] if isinstance(src_b, list) else src_b
            P.act(junk, junk[:, 0:width], sb_c, src_ap_fn(c), AF.Square, accum=ssb[:, c:c + 1], extra_w=[ssb])
        P.ts("dve", ssb, ssb[:, 0:nch], ssb, ssb[:, 0:nch], 1.0 / width, EPS, ALU.mult, ALU.add)
        P.act(ssb, ssb[:, 0:nch], ssb, ssb[:, 0:nch], AF.Sqrt)
        P.C.op("dve", [ssb], [ssb], lambda: nc.vector.reciprocal(out=ssb[:, 0:nch], in_=ssb[:, 0:nch]))

    def norm_to_hT(ph_bufs, xt, nch, layer, src, sub, inplace=True, xk=None):
        junk, ssb, hT, hTk = ph_bufs["junk"], ph_bufs["ss"], ph_bufs["hT"], ph_bufs["hTk"]
        T = nch * 128
        xk = xk if xk is not None else [xt] * 4
        row_rstd(junk, ssb, list(xk), lambda c: xt[:, c, :], nch, D)
        if inplace:
            for c in range(nch):
                P.act(xk[c], xt[:, c, :], xk[c], xt[:, c, :], AF.Identity, scale=ssb[:, c:c + 1], extra_r=[ssb])
        else:
            diag = ph_bufs["diag"]
            for c in range(nch):
                P.ts("dve", diag, diag[:, c, :], ident_f, ident_f[:], ssb[:, c:c + 1], None, ALU.mult, extra_r=[ssb])
        for kc in range(16):
            pt = pb[6 + (kc % 2)]

            def emit(kc=kc, pt=pt):
                last = None
                for c in range(nch):
                    if inplace:
                        last = nc.tensor.transpose(out=pt[:, c * 128:(c + 1) * 128], in_=xt[:, c, kc * 128:(kc + 1) * 128], identity=ident_f[:])
                    else:
                        last = nc.tensor.matmul(pt[:, c * 128:(c + 1) * 128], lhsT=xt[:, c, kc * 128:(kc + 1) * 128], rhs=diag[:, c, :],
                                                start=True, stop=True)
                return last
            P.mm(pt, list(dict.fromkeys(xk[:nch])) + [ident_f] + ([] if inplace else [ph_bufs["diag"]]), emit)
            a_ap = colA[:, layer, src, sub, kc:kc + 1]
            b_ap = colB[:, layer, src, sub, kc:kc + 1]
            if kc % 2 == 0:
                P.act(hTk[kc], hT[:, kc, 0:T], pt, pt[:, 0:T], AF.Identity, bias=b_ap, scale=a_ap, extra_r=[colA, colB])
            else:
                P.ts("dve", hTk[kc], hT[:, kc, 0:T], pt, pt[:, 0:T], a_ap, b_ap, ALU.mult, ALU.add, extra_r=[colA, colB])
        return hTk

    UT4 = P.dscr("UT4", [4, 1024, NALL // 4], BF16)
    UTf = P.dscr("UTf", [1024, NFULL], F32)
    QT = P.dscr("QT", [1024, NFULL], BF16)
    KT = P.dscr("KT", [1024, NFULL], BF16)
    GT = P.dscr("GT", [1024, NFULL], F32)
    Ktm = P.dscr("Ktm", [NALL, 1024], BF16)
    Vtm = P.dscr("Vtm", [NALL, 1024], BF16)

    tiles = [("ctx", 0, 2, True, 0)]
    for t0 in range(0, 2048, 512):
        tiles.append(("lat", t0, 4, True, NCTX + t0))
    tiles.append(("lat", 2048, 1, True, NCTX + 2048))
    t0 = 2176
    while t0 < L:
        n = min(4, (L - t0) // 128)
        tiles.append(("lat", t0, n, False, NCTX + t0))
        t0 += n * 128

    def phase1():
        ph = Phase()
        ph.wpool()
        xt = ph.sb("xt", [128, 4, D], F32)
        bufs = {"junk": ph.sb("junk", [128, D], BF16), "ss": ph.sb("ss", [128, 4], F32)}
        hTs = [ph.sb("hT%d" % i, [128, 16, 512], BF16) for i in range(2)]
        hTks = [[Buf("hT%d_%d" % (i, k)) for k in range(16)] for i in range(2)]
        st_b = [ph.sb("stb%d" % i, [128, 4, 512], BF16) for i in range(3)]
        st_f = [ph.sb("stf%d" % i, [128, 4, 512], F32) for i in range(2)]
        cnt = {"ps": 0, "sb": 0, "sf": 0, "ev": 0}
        for ti, (srcname, r0, nch, full, tt0) in enumerate(tiles[:int(os.environ.get('K_P1_TILES', '99'))]):
            T = nch * 128
            src = 1 if srcname == "ctx" else 0
            xin = ctx_in if srcname == "ctx" else x_in
            P.dma(P.ldq(), xt, xt[:, 0:nch, :], xin, xin[r0:r0 + T, :].rearrange("(c p) d -> p c d", p=128))
            bufs["hT"] = hTs[ti % 2]
            bufs["hTk"] = hTks[ti % 2]
            hT = hTs[ti % 2]
            hTk = norm_to_hT(bufs, xt, nch, 0, src, 0)
            for cb in range(int(os.environ.get('K_P1_CBS', '10'))):
                kind = ["u", "q", "k", "v", "g"][cb // 2]
                if not full and kind in ("q", "g"):
                    continue
                wb = wload(w_in0[:, cb * 512:(cb + 1) * 512].rearrange("(kc p) n -> p kc n", p=128))
                half = cb % 2
                if kind in ("u", "q", "g") or (kind == "k" and full):
                    sb_ = st_b[cnt["sb"] % 3]
                    cnt["sb"] += 1
                    sf_ = None
                    if full and kind in ("u", "g"):
                        sf_ = st_f[cnt["sf"] % 2]
                        cnt["sf"] += 1
                    for m in range(4):
                        pp = pb[cnt["ps"] % 6]
                        cnt["ps"] += 1

                        def emit(wb=wb, m=m, pp=pp, hT=hT, T=T):
                            last = None
                            for kc in range(16):
                                last = nc.tensor.matmul(pp[:, 0:T], lhsT=wb[:, kc, m * 128:(m + 1) * 128], rhs=hT[:, kc, 0:T],
                                                        start=(kc == 0), stop=(kc == 15))
                            return last
                        P.mm(pp, [wb] + hTk, emit)
                        if kind == "u":
                            P.copy("act" if cnt["ev"] % 2 else "dve", sb_, sb_[:, m, 0:T].rearrange("p (s j) -> p s j", s=4),
                                   pp, pp[:, 0:T].rearrange("p (j s) -> p s j", s=4))
                            cnt["ev"] += 1
                        elif kind != "g":
                            P.copy("act" if cnt["ev"] % 2 else "dve", sb_, sb_[:, m, 0:T], pp, pp[:, 0:T])
                            cnt["ev"] += 1
                        if sf_ is not None:
                            P.copy("act" if cnt["ev"] % 2 else "dve", sf_, sf_[:, m, 0:T], pp, pp[:, 0:T])
                            cnt["ev"] += 1
                    rows = slice(half * 512, (half + 1) * 512)
                    if kind == "u":
                        for s4 in range(4):
                            P.dma(P.ldq(), UT4, UT4[s4, rows, tt0 // 4:(tt0 + T) // 4].rearrange("(m p) j -> p m j", p=128),
                                  sb_, sb_[:, :, s4 * (T // 4):(s4 + 1) * (T // 4)])
                        if full:
                            P.dma(P.ldq(), UTf, UTf[rows, tt0:tt0 + T].rearrange("(m p) t -> p m t", p=128), sf_, sf_[:, :, 0:T])
                    elif kind == "q":
                        P.dma(P.ldq(), QT, QT[rows, tt0:tt0 + T].rearrange("(m p) t -> p m t", p=128), sb_, sb_[:, :, 0:T])
                    elif kind == "k":
                        P.dma(P.ldq(), KT, KT[rows, tt0:tt0 + T].rearrange("(m p) t -> p m t", p=128), sb_, sb_[:, :, 0:T])
                    elif kind == "g":
                        P.dma(P.ldq(), GT, GT[rows, tt0:tt0 + T].rearrange("(m p) t -> p m t", p=128), sf_, sf_[:, :, 0:T])
                if kind in ("k", "v"):
                    sb_ = st_b[cnt["sb"] % 3]
                    cnt["sb"] += 1
                    for c in range(nch):
                        pp = pb[cnt["ps"] % 6]
                        cnt["ps"] += 1

                        def emit(wb=wb, c=c, pp=pp, hT=hT):
                            last = None
                            for kc in range(16):
                                last = nc.tensor.matmul(pp[:, :], lhsT=hT[:, kc, c * 128:(c + 1) * 128], rhs=wb[:, kc, :],
                                                        start=(kc == 0), stop=(kc == 15))
                            return last
                        P.mm(pp, [wb] + hTk, emit)
                        P.copy("act" if cnt["ev"] % 2 else "dve", sb_, sb_[:, c, :], pp, pp[:, :])
                        cnt["ev"] += 1
                    dst = Ktm if kind == "k" else Vtm
                    P.dma(P.ldq(), dst, dst[tt0:tt0 + T, half * 512:(half + 1) * 512].rearrange("(c p) n -> p c n", p=128), sb_, sb_[:, 0:nch, :])
        ph.close()

    if not os.environ.get('K_SKIP_P1'):
        phase1()
    if stop_after == "p1":
        return finish(P)

    ret_dmask = P.din("ret_dmask", [128, 2, 4, 128])
    ret_qdec = P.din("ret_qdec", [2, 4, 128])
    ret_kdec = P.din("ret_kdec", [128, 2, 4])
    ret_cdec = P.din("ret_cdec", [128, 2, 4])
    OA = P.dscr("OA", [1024, NFULL], F32)
    OT = P.dscr("OT", [1024, NFULL], F32)

    orderA = [(c, True) for c in range(0, 19)]
    orderB = [(1, True), (0, True)] + [(2 + c, False) for c in range(31, 16, -1)] + [(2 + c, True) for c in range(16, -1, -1)]

    def phase_ret():
        ph = Phase()
        dmask = ph.sb("dmask", [128, 2, 4, 128], F32)
        qdec = ph.sb("qdec", [128, 2, 4, 128], F32)
        kdec = ph.sb("kdec", [128, 2, 4], F32)
        cdec = ph.sb("cdec", [128, 2, 4], F32)
        P.dma("sp", dmask, dmask[:], ret_dmask, ret_dmask[:])
        P.dma("act", qdec, qdec[:], ret_qdec, ret_qdec[:].partition_broadcast(128))
        P.dma("sp", kdec, kdec[:], ret_kdec, ret_kdec[:])
        P.dma("act", cdec, cdec[:], ret_cdec, ret_cdec[:])
        R = ph.sb("R", [128, 4, 2, 256], F32)
        Rb = ph.sb("Rb", [128, 4, 2, 256], BF16)
        Rh = [Buf("R%d" % h) for h in range(4)]
        nb = 2
        qTs = [ph.sb("qT%d" % i, [128, 8, 128], BF16) for i in range(nb)]
        kTs = [ph.sb("kT%d" % i, [128, 8, 128], BF16) for i in range(nb)]
        kts = [ph.sb("ktm%d" % i, [128, 4, 256], BF16) for i in range(nb)]
        vts = [ph.sb("vtm%d" % i, [128, 4, 256], BF16) for i in range(nb)]
        oas = [ph.sb("oa%d" % i, [128, 8, 128], F32) for i in range(nb)]
        osb = [ph.sb("osb%d" % i, [128, 8, 128], F32) for i in range(nb)]
        ks = ph.sb("ks", [128, 4, 256], BF16)
        qs = ph.sb("qs", [128, 8, 128], BF16)
        sTb = ph.sb("sTb", [128, 4, 128], BF16)
        it = 0
        for X, order in ((0, orderA), (1, orderB)):
            if X == 1:
                C.barrier()
            C.op("dve", [], [R] + Rh, lambda: nc.vector.memset(R[:], 0.0))
            P.memset("dve", Rb, Rb[:], 0.0)
            for (cid, full) in order:
                bi = it % nb
                it += 1
                t0 = cid * 128
                qT, kT, kt, vt, oa, ob = qTs[bi], kTs[bi], kts[bi], vts[bi], oas[bi], osb[bi]
                P.dma(P.ldq(), kt, kt[:], Ktm, Ktm[t0:t0 + 128, :].rearrange("t (h d) -> t h d", h=4))
                P.dma(P.ldq(), vt, vt[:], Vtm, Vtm[t0:t0 + 128, :].rearrange("t (h d) -> t h d", h=4))
                if full:
                    P.dma(P.ldq(), qT, qT[:], QT, QT[:, t0:t0 + 128].rearrange("(j p) t -> p j t", p=128))
                    P.dma(P.ldq(), kT, kT[:], KT, KT[:, t0:t0 + 128].rearrange("(j p) t -> p j t", p=128))
                    if X == 1:
                        P.dma(P.ldq(), oa, oa[:], OA, OA[:, t0:t0 + 128].rearrange("(j p) t -> p j t", p=128))
                P.tt("pool", ks, ks[:], kt, kt[:], kdec, kdec[:, X, :].unsqueeze(2).broadcast_to([128, 4, 256]), ALU.mult)
                if full:
                    P.tt("pool", qs, qs[:].rearrange("p (h c) t -> p h c t", h=4), qT, qT[:].rearrange("p (h c) t -> p h c t", h=4),
                         qdec, qdec[:, X, :, :].unsqueeze(2).broadcast_to([128, 4, 2, 128]), ALU.mult)
                    sp = pb[0]

                    def emit_s(kT=kT, qT=qT, sp=sp):
                        last = None
                        for h in range(4):
                            for dc in range(2):
                                last = nc.tensor.matmul(sp[:, h * 128:(h + 1) * 128], lhsT=kT[:, 2 * h + dc, :], rhs=qT[:, 2 * h + dc, :],
                                                        start=(dc == 0), stop=(dc == 1))
                        return last
                    P.mm(sp, [kT, qT], emit_s)
                    P.tt("dve", sTb, sTb[:], sp, sp[:, :].rearrange("p (h n) -> p h n", h=4), dmask, dmask[:, X, :, :], ALU.mult)
                    for hp in range(2):
                        op_ = pb[1 + hp]

                        def emit_o(hp=hp, op_=op_, vt=vt):
                            last = None
                            for hh in range(2):
                                h = 2 * hp + hh
                                for ec in range(2):
                                    o_ap = op_[:, (hh * 2 + ec) * 128:(hh * 2 + ec + 1) * 128]
                                    nc.tensor.matmul(o_ap, lhsT=vt[:, h, ec * 128:(ec + 1) * 128], rhs=sTb[:, h, :], start=True, stop=False)
                                    for dc in range(2):
                                        last = nc.tensor.matmul(o_ap, lhsT=Rb[:, h, dc, ec * 128:(ec + 1) * 128], rhs=qs[:, 2 * h + dc, :],
                                                                start=False, stop=(dc == 1))
                            return last
                        P.mm(op_, [vt, sTb, Rb, qs], emit_o)
                        o_dst = ob[:, hp * 4:(hp + 1) * 4, :]
                        o_src = op_[:, :].rearrange("p (j n) -> p j n", j=4)
                        if X == 0:
                            P.copy("act", ob, o_dst, op_, o_src)
                        else:
                            P.tt("dve", ob, o_dst, op_, o_src, oa, oa[:, hp * 4:(hp + 1) * 4, :], ALU.add)
                    dst = OA if X == 0 else OT
                    P.dma(P.ldq(), dst, dst[:, t0:t0 + 128].rearrange("(j p) t -> p j t", p=128), ob, ob[:])
                for h in range(4):
                    rp = pb[3 + h]

                    def emit_r(h=h, rp=rp, vt=vt):
                        last = None
                        for dc in range(2):
                            last = nc.tensor.matmul(rp[:, dc * 256:(dc + 1) * 256], lhsT=ks[:, h, dc * 128:(dc + 1) * 128], rhs=vt[:, h, :],
                                                    start=True, stop=True)
                        return last
                    P.mm(rp, [ks, vt, Rb], emit_r)
                    P.stt(Rh[h], R[:, h, :, :], Rh[h], R[:, h, :, :], cdec[:, X, h:h + 1], rp, rp[:, :].rearrange("p (c e) -> p c e", c=2),
                          ALU.mult, ALU.add, extra_r=[cdec])
                P.C.op("act", Rh, [Rb], lambda: nc.scalar.copy(out=Rb[:], in_=R[:]))
        ph.close()

    w_out0 = P.din("even_w_out", [D, D])
    glu_w = P.din("glu_w", [1024, 1024])
    mlp_w1 = P.din("mlp_w1", [2, D, 4 * D])
    mlp_w2 = P.din("mlp_w2", [2, 4 * D, D])
    w_in1 = P.din("odd_w_in_ext", [D, 4096])
    w_out1 = P.din("odd_w_out", [D, D])
    if not os.environ.get('K_SKIP_RET'):
        phase_ret()
    if stop_after == "ret":
        return finish(P)

    s5_lre = P.din("s5_lre", [128, 64])
    s5_lim = P.din("s5_lim", [128, 64])
    s5_ldt = P.din("s5_ldt", [128, 64])
    s5_B = P.din("s5_B", [128, 64, 2, 32])
    s5_C = P.din("s5_C", [128, 64, 2, 32])
    YA = P.dscr("YA", [1024, NFULL], F32)
    YB = P.dscr("YB", [1024, NFULL], F32)
    TWO_PI = 2.0 * np.pi
    C1 = float(np.float32(TWO_PI))
    C2 = float(TWO_PI - np.float64(np.float32(TWO_PI)))
    MAGIC = 12582912.0

    def phase_s5():
        ph = Phase()
        SS = 4
        WinR = ph.sb("WinR", [128, 64, 2, 128], BF16)
        CAt = ph.sb("CAt", [128, 64, 4, 2, 32], BF16)
        Ktb = ph.sb("Ktb", [128, 64, 4, 32], BF16)
        AR2 = ph.sb("AR2", [128, 2, 64], F32)
        AI2 = ph.sb("AI2", [128, 2, 64], F32)
        zeros_f = ph.sb("zeros_f", [128, 32], F32)
        P.memset("dve", zeros_f, zeros_f[:], 0.0)
        ph2 = Phase()
        tn = ["lr", "li", "dt", "zr", "zi", "mag", "sn", "cs", "den", "nr", "fr", "fi", "t1", "t2", "kk"]
        tb = {n: ph2.sb("s5t_" + n, [128, 64], F32) for n in tn}
        pw = [(ph2.sb("s5pr%d" % m, [128, 64], F32), ph2.sb("s5pi%d" % m, [128, 64], F32)) for m in range(5)]
        P.dma("sp", tb["lr"], tb["lr"][:], s5_lre, s5_lre[:])
        P.dma("act", tb["li"], tb["li"][:], s5_lim, s5_lim[:])
        P.dma("sp", tb["dt"], tb["dt"][:], s5_ldt, s5_ldt[:])
        V = lambda n: tb[n][:]
        def TS(o, a, s1, s2, op0, op1=None): P.ts("dve", tb[o], V(o), tb[a], V(a), s1, s2, op0, op1)
        def TT(o, a, b, op): P.tt("dve", tb[o], V(o), tb[a], V(a), tb[b], V(b), op)
        def STT(o, a, sc, b, op0, op1): P.stt(tb[o], V(o), tb[a], V(a), sc, tb[b], V(b), op0, op1)
        TS("lr", "lr", -1e-4, None, ALU.min)
        P.act(tb["dt"], V("dt"), tb["dt"], V("dt"), AF.Exp)
        TT("zr", "lr", "dt", ALU.mult)
        TT("zi", "li", "dt", ALU.mult)
        P.act(tb["mag"], V("mag"), tb["zr"], V("zr"), AF.Exp)

        def sin_of(dst, src, shift):
            TS("t1", src, shift, None, ALU.add)
            TS("kk", "t1", 1.0 / TWO_PI, None, ALU.mult)
            TS("kk", "kk", MAGIC, None, ALU.add)
            TS("kk", "kk", -MAGIC, None, ALU.add)
            STT("t1", "kk", -C1, "t1", ALU.mult, ALU.add)
            STT("t1", "kk", -C2, "t1", ALU.mult, ALU.add)
            TS("t1", "t1", 3.1415925, -3.1415925, ALU.min, ALU.max)
            P.act(tb[dst], V(dst), tb["t1"], V("t1"), AF.Sin)
        sin_of("sn", "zi", 0.0)
        sin_of("cs", "zi", float(np.pi / 2))
        ar, ai = pw[1]
        P.tt("dve", ar, ar[:], tb["mag"], V("mag"), tb["cs"], V("cs"), ALU.mult)
        P.tt("dve", ai, ai[:], tb["mag"], V("mag"), tb["sn"], V("sn"), ALU.mult)
        TT("den", "lr", "lr", ALU.mult)
        TT("t1", "li", "li", ALU.mult)
        TT("den", "den", "t1", ALU.add)
        C.op("dve", [tb["den"]], [tb["den"]], lambda: nc.vector.reciprocal(out=V("den"), in_=V("den")))
        P.ts("dve", tb["nr"], V("nr"), ar, ar[:], -1.0, None, ALU.add)
        TT("t1", "nr", "lr", ALU.mult)
        P.tt("dve", tb["t2"], V("t2"), ai, ai[:], tb["li"], V("li"), ALU.mult)
        TT("fr", "t1", "t2", ALU.add)
        TT("fr", "fr", "den", ALU.mult)
        P.tt("dve", tb["t1"], V("t1"), ai, ai[:], tb["lr"], V("lr"), ALU.mult)
        TT("t2", "nr", "li", ALU.mult)
        TT("fi", "t1", "t2", ALU.subtract)
        TT("fi", "fi", "den", ALU.mult)
        P.memset("dve", pw[0][0], pw[0][0][:], 1.0)
        P.memset("dve", pw[0][1], pw[0][1][:], 0.0)
        for m in range(2, 5):
            pr, pi = pw[m]
            qr, qi = pw[m - 1]
            P.tt("dve", pr, pr[:], qr, qr[:], ar, ar[:], ALU.mult)
            P.tt("dve", tb["t1"], V("t1"), qi, qi[:], ai, ai[:], ALU.mult)
            P.tt("dve", pr, pr[:], pr, pr[:], tb["t1"], V("t1"), ALU.subtract)
            P.tt("dve", pi, pi[:], qr, qr[:], ai, ai[:], ALU.mult)
            P.tt("dve", tb["t1"], V("t1"), qi, qi[:], ar, ar[:], ALU.mult)
            P.tt("dve", pi, pi[:], pi, pi[:], tb["t1"], V("t1"), ALU.add)
        P.copy("dve", AR2, AR2[:, 0, :], pw[4][0], pw[4][0][:])
        P.copy("dve", AR2, AR2[:, 1, :], pw[4][0], pw[4][0][:])
        P.ts("dve", AI2, AI2[:, 0, :], pw[4][1], pw[4][1][:], -1.0, None, ALU.mult)
        P.copy("dve", AI2, AI2[:, 1, :], pw[4][1], pw[4][1][:])
        Bc = ph2.sb("s5Bc", [128, 32, 2, 32], F32)
        Cc = ph2.sb("s5Cc", [128, 32, 2, 32], F32)
        Bb = ph2.sb("s5Bb", [128, 32, 2, 32], F32)
        tA = ph2.sb("s5tA", [128, 32, 32], F32)
        tB = ph2.sb("s5tB", [128, 32, 32], F32)
        WinC = ph2.sb("s5WinC", [128, 32, 2, 4, 32], F32)
        CAf = ph2.sb("s5CAf", [128, 32, 4, 2, 32], F32)
        CA4 = ph2.sb("s5CA4", [128, 32, 2, 32], F32)
        for X in range(2):
            sl = slice(X * 32, X * 32 + 32)
            P.dma("act", Bc, Bc[:], s5_B, s5_B[:, sl, :, :])
            P.dma("sp", Cc, Cc[:], s5_C, s5_C[:, sl, :, :])
            bcx = lambda buf: buf[:, sl].unsqueeze(2).broadcast_to([128, 32, 32])

            def cmul(dst_b, dst_re, dst_im, src_b, s_re, s_im, a_re, a_im, neg_im=False):
                P.tt("dve", dst_b, dst_re, src_b, s_re, a_re, bcx(a_re), ALU.mult)
                P.tt("dve", tA, tA[:], src_b, s_im, a_im, bcx(a_im), ALU.mult)
                P.tt("dve", dst_b, dst_re, dst_b, dst_re, tA, tA[:], ALU.subtract)
                P.tt("dve", dst_b, dst_im, src_b, s_re, a_im, bcx(a_im), ALU.mult)
                P.tt("dve", tB, tB[:], src_b, s_im, a_re, bcx(a_re), ALU.mult)
                if neg_im:
                    P.stt(dst_b, dst_im, dst_b, dst_im, -1.0, tB, tB[:], ALU.mult, ALU.subtract)
                else:
                    P.tt("dve", dst_b, dst_im, dst_b, dst_im, tB, tB[:], ALU.add)
            cmul(Bb, Bb[:, :, 0, :], Bb[:, :, 1, :], Bc, Bc[:, :, 0, :], Bc[:, :, 1, :], tb["fr"], tb["fi"])
            for s4 in range(4):
                e = (3 - s4) if X == 0 else s4
                cmul(WinC, WinC[:, :, 0, s4, :], WinC[:, :, 1, s4, :], Bb, Bb[:, :, 0, :], Bb[:, :, 1, :], pw[e][0], pw[e][1])
            for lag in range(4):
                cmul(CAf, CAf[:, :, lag, 0, :], CAf[:, :, lag, 1, :], Cc, Cc[:, :, 0, :], Cc[:, :, 1, :], pw[lag][0], pw[lag][1], neg_im=True)
            cmul(CA4, CA4[:, :, 0, :], CA4[:, :, 1, :], Cc, Cc[:, :, 0, :], Cc[:, :, 1, :], pw[4][0], pw[4][1], neg_im=True)
            for t4 in range(4):
                m = (t4 + 1) if X == 0 else (4 - t4)
                if m == 4:
                    P.copy("act", CAt, CAt[:, sl, t4, :, :], CA4, CA4[:])
                else:
                    P.copy("act", CAt, CAt[:, sl, t4, :, :], CAf, CAf[:, :, m, :, :])
            for g4 in range(16):
                pp = pb[g4 % 4]

                def emit(g4=g4, pp=pp):
                    last = None
                    for jj in range(4):
                        idx = g4 * 4 + jj
                        pl_, ri = idx // 2, idx % 2
                        last = nc.tensor.transpose(out=pp[:, jj * 128:(jj + 1) * 128], in_=WinC[:, pl_, ri, :, :], identity=ident_f[:])
                    return last
                P.mm(pp, [WinC, ident_f], emit)
                pd0 = X * 32 + g4 * 2
                P.copy("act", WinR, WinR[:, pd0:pd0 + 2, :, :], pp, pp[:, :].rearrange("p (a r n) -> p a r n", a=2, r=2))
            for g4 in range(8):
                pp = pb[4 + g4 % 2]

                def emit(g4=g4, pp=pp, X=X):
                    last = None
                    for a in range(4):
                        pl_ = g4 * 4 + a
                        for t4 in range(4):
                            for s4 in range(4):
                                lag = (t4 - s4) if X == 0 else (s4 - t4)
                                o_ap = pp[32 * s4:32 * s4 + 32, (a * 4 + t4) * 32:(a * 4 + t4 + 1) * 32]
                                kw = {"tile_position": (0, 96)} if s4 == 3 else {}
                                if lag < 0:
                                    last = nc.tensor.matmul(o_ap, lhsT=Bb[:, pl_, 0, :], rhs=zeros_f[:, :], start=True, stop=True, **kw)
                                else:
                                    nc.tensor.matmul(o_ap, lhsT=Bb[:, pl_, 0, :], rhs=CAf[:, pl_, lag, 0, :], start=True, stop=False, **kw)
                                    last = nc.tensor.matmul(o_ap, lhsT=Bb[:, pl_, 1, :], rhs=CAf[:, pl_, lag, 1, :], start=False, stop=True, **kw)
                    return last
                P.mm(pp, [Bb, CAf, zeros_f], emit)
                pd0 = X * 32 + g4 * 4
                P.copy("act", Ktb, Ktb[:, pd0:pd0 + 4, :, :], pp, pp[:, :].rearrange("p (a t q) -> p a t q", a=4, t=4))
        ph2.close()

        J = 32
        Hhs = [ph.sb("Hh%d" % i, [128, J, 2, 64], F32) for i in range(2)]
        Hb = ph.sb("Hb", [128, J, 2, 64], BF16)
        Hc = ph.sb("Hc", [128, 2, 64], F32)
        Xbs = [ph.sb("Xb%d" % i, [128, J, 2, 64], F32) for i in range(2)]
        Ups = [[ph.sb("Up%d_%d" % (X, i), [128, 32, J], BF16) for i in range(2)] for X in range(2)]
        tmp1 = ph.sb("tmp1", [128, 2, 64], F32)
        tmp2 = ph.sb("tmp2", [128, 2, 64], F32)
        ysb = [[ph.sb("ysb%d_%d" % (X, i), [128, 8, 128], F32) for i in range(2)] for X in range(2)]
        P.memset("dve", Hc, Hc[:], 0.0)
        nsteps = len(orderB)

        def s5_setup(j):
            doA = j < len(orderA)
            dirs = [0, 1] if doA else [1]
            lo, hi = (0, 64) if doA else (32, 64)
            info = {X: (orderA[j] if X == 0 else orderB[j]) for X in dirs}
            Xb, Hh = Xbs[j % 2], Hhs[j % 2]
            return doA, dirs, lo, hi, info, Xb, Hh

        def s5_loads(j):
            doA, dirs, lo, hi, info, Xb, Hh = s5_setup(j)
            for X in dirs:
                cid = info[X][0]
                Up = Ups[X][j % 2]
                for s4 in range(4):
                    P.dma(P.ldq(), Up, Up[32 * s4:32 * s4 + 32, :, :], UT4, UT4[s4, :, cid * J:(cid + 1) * J].rearrange("(pr p) j -> p pr j", p=32))

        def s5_x(j):
            doA, dirs, lo, hi, info, Xb, Hh = s5_setup(j)
            for X in dirs:
                Up = Ups[X][j % 2]
                for g4 in range(4):
                    pp = pb[g4 % 4]

                    def emit(g4=g4, pp=pp, Up=Up, X=X):
                        last = None
                        for jj in range(16):
                            pair, ri = g4 * 8 + jj // 2, jj % 2
                            last = nc.tensor.matmul(pp[:, jj * J:(jj + 1) * J], lhsT=WinR[:, X * 32 + pair, ri, :], rhs=Up[:, pair, :],
                                                    start=True, stop=True)
                        return last
                    P.mm(pp, [WinR, Up], emit)
                    pd0 = X * 32 + g4 * 8
                    src = pp[:, :].rearrange("p (a r t) -> p a r t", a=8, r=2)
                    if X == 0:
                        dst = Xb[:, :, :, pd0:pd0 + 8].rearrange("p t r a -> p a r t")
                    else:
                        dst = Xb[:, ::-1, :, pd0:pd0 + 8].rearrange("p t r a -> p a r t")
                    P.copy("act", Xb, dst, pp, src)

        s5_loads(0)
        s5_x(0)
        for j in range(nsteps):
            doA, dirs, lo, hi, info, Xb, Hh = s5_setup(j)
            if j + 1 < nsteps:
                s5_loads(j + 1)
            for _ in range(3):
                precast_step()
            need_out = [X for X in dirs if info[X][1]]
            if need_out:
                for X in need_out:
                    sl = slice(X * 32, X * 32 + 32)
                    P.copy("act", Hb, Hb[:, 0 if X == 0 else J - 1, :, sl], Hc, Hc[:, :, sl])
            if j + 1 < nsteps:
                s5_x(j + 1)
            for i in range(J):
                pbuf = Hc if i == 0 else Hh
                prev = Hc[:, :, lo:hi] if i == 0 else Hh[:, i - 1, :, lo:hi]
                prev_sw = Hc[:, ::-1, lo:hi] if i == 0 else Hh[:, i - 1, ::-1, lo:hi]
                P.tt("dve", tmp1, tmp1[:, :, lo:hi], pbuf, prev, AR2, AR2[:, :, lo:hi], ALU.mult)
                P.tt("dve", tmp2, tmp2[:, :, lo:hi], pbuf, prev_sw, AI2, AI2[:, :, lo:hi], ALU.mult)
                P.tt("dve", tmp1, tmp1[:, :, lo:hi], tmp1, tmp1[:, :, lo:hi], tmp2, tmp2[:, :, lo:hi], ALU.add)
                P.tt("dve", Hh, Hh[:, i, :, lo:hi], tmp1, tmp1[:, :, lo:hi], Xb, Xb[:, i, :, lo:hi], ALU.add)
            P.copy("dve", Hc, Hc[:, :, lo:hi], Hh, Hh[:, J - 1, :, lo:hi])
            for X in need_out:
                sl = slice(X * 32, X * 32 + 32)
                if X == 0:
                    P.copy("act", Hb, Hb[:, 1:J, :, sl], Hh, Hh[:, 0:J - 1, :, sl])
                else:
                    P.copy("act", Hb, Hb[:, 0:J - 1, :, sl], Hh, Hh[:, J - 2::-1, :, sl] if False else Hh[:, 0:J - 1, :, sl][:, ::-1, :, :])
                Up = Ups[X][j % 2]
                yb = ysb[X][j % 2]
                for hf in range(2):
                    pp = pb[4 + hf + 2 * X]

                    def emit(pp=pp, X=X, hf=hf, Up=Up):
                        last = None
                        for k4 in range(4):
                            kt = hf * 4 + k4
                            for pl in range(4):
                                pair = kt * 4 + pl
                                pd = X * 32 + pair
                                kw = {"tile_position": (0, 96)} if pl == 3 else {}
                                for t4 in range(4):
                                    o_ap = pp[32 * pl:32 * pl + 32, (k4 * 4 + t4) * J:(k4 * 4 + t4 + 1) * J]
                                    nc.tensor.matmul(o_ap, lhsT=Ktb[:, pd, t4, :], rhs=Up[:, pair, :], start=True, stop=False, **kw)
                                    nc.tensor.matmul(o_ap, lhsT=CAt[:, pd, t4, 0, :], rhs=Hb[:, :, 0, pd], start=False, stop=False, **kw)
                                    last = nc.tensor.matmul(o_ap, lhsT=CAt[:, pd, t4, 1, :], rhs=Hb[:, :, 1, pd], start=False, stop=True, **kw)
                        return last
                    P.mm(pp, [Ktb, CAt, Hb, Up], emit)
                    P.copy("act", yb, yb[:, hf * 4:hf * 4 + 4, :].rearrange("p k (j s) -> p k s j", s=4), pp, pp[:, :].rearrange("p (k s j) -> p k s j", k=4, s=4))
                cid = info[X][0]
                dst = YA if X == 0 else YB
                P.dma(P.ldq(), dst, dst[:, cid * 128:(cid + 1) * 128].rearrange("(k p) t -> p k t", p=128), yb, yb[:])
        while precast_step():
            pass
        ph.close()

    if not os.environ.get('K_NO_PRECAST'):
        w_out0 = precast(w_out0, "w_out0_b")
        glu_w = precast(glu_w, "glu_w_b")
        mlp_w1 = precast(mlp_w1, "mlp_w1_b")
        mlp_w2 = precast(mlp_w2, "mlp_w2_b")
        w_in1 = precast(w_in1, "w_in1_b")
        w_out1 = precast(w_out1, "w_out1_b")
    if not os.environ.get('K_SKIP_S5'):
        phase_s5()
    if stop_after == "s5":
        return finish(P)

    glub_col = P.din("glub_col", [128, 8])
    dskip_col = P.din("dskip_col", [128, 8])
    X1a = P.dscr("X1a", [NFULL, D], F32)
    X1 = P.dscr("X1", [NFULL, D], F32)
    ones_b = P.sb("ones_b", [128, 128], BF16)
    P.memset("dve", ones_b, ones_b[:], 1.0)

    def mlp_tile(ph_bufs, xbuf, nch, layer, src, x1_dram_rows, out_dram, out_rows, GG, ggsrc, xk=None):
        T = nch * 128
        aT, BaT, tb, x1rs, ssb, junk = ph_bufs["aT"], ph_bufs["BaT"], ph_bufs["tb"], ph_bufs["x1r"], ph_bufs["ss"], ph_bufs["junk"]
        hT = ph_bufs["hT"]
        xk = xk if xk is not None else [xbuf] * 4
        xb_, xr0 = x1_dram_rows
        hTk = norm_to_hT(ph_bufs, xbuf, nch, layer, src, 1, inplace=True, xk=xk)
        P.dma("sp", GG, GG[:], GGS, ggsrc.partition_broadcast(128))
        k = 0
        for cb in range(16):
            wb = wload(mlp_w1[layer, :, cb * 512:(cb + 1) * 512].rearrange("(kc p) n -> p kc n", p=128))
            for m in range(4):
                pp = pb[k % 6]
                t_ = tb[k % len(tb)]
                k += 1

                def emit(wb=wb, m=m, pp=pp):
                    last = None
                    for kc in range(16):
                        last = nc.tensor.matmul(pp[:, 0:T], lhsT=wb[:, kc, m * 128:(m + 1) * 128], rhs=hT[:, kc, 0:T],
                                                start=(kc == 0), stop=(kc == 15))
                    return last
                P.mm(pp, [wb] + hTk, emit)
                P.act(t_, t_[:, 0:T], pp, pp[:, 0:T], AF.Relu)
                mo = cb * 4 + m
                P.tt("pool" if k % 2 else "dve", BaT[mo], aT[:, mo, 0:T], t_, t_[:, 0:T], t_, t_[:, 0:T], ALU.mult)
        for c in range(min(nch, len(x1rs))):
            P.dma("act", x1rs[c], x1rs[c][:], xb_, xb_[xr0 + c * 128:xr0 + (c + 1) * 128, :])
        ev = 0
        for cb in range(4):
            for ks in range(4):
                wb = wload(mlp_w2[layer, ks * 2048:(ks + 1) * 2048, cb * 512:(cb + 1) * 512].rearrange("(kc p) n -> p kc n", p=128))
                for c in range(nch):
                    pp = pb[c]

                    def emit(wb=wb, c=c, pp=pp, ks=ks):
                        last = None
                        for kc in range(16):
                            last = nc.tensor.matmul(pp[:, :], lhsT=aT[:, ks * 16 + kc, c * 128:(c + 1) * 128], rhs=wb[:, kc, :],
                                                    start=(ks == 0 and kc == 0), stop=(ks == 3 and kc == 15))
                        return last
                    P.mm(pp, [wb] + BaT[ks * 16:(ks + 1) * 16], emit)
            for c in range(nch):
                P.copy("act" if ev % 2 else "dve", xk[c], xbuf[:, c, cb * 512:(cb + 1) * 512], pb[c], pb[c][:, :])
                ev += 1
        row_rstd(junk, ssb, list(xk), lambda c: xbuf[:, c, :], nch, D)
        for c in range(nch):
            x1r = x1rs[c % len(x1rs)]
            if c >= len(x1rs):
                P.dma("act", x1r, x1r[:], xb_, xb_[xr0 + c * 128:xr0 + (c + 1) * 128, :])
            P.stt(xk[c], xbuf[:, c, :], xk[c], xbuf[:, c, :], ssb[:, c:c + 1], GG, GG[:], ALU.mult, ALU.mult, extra_r=[ssb])
            P.tt("dve", xk[c], xbuf[:, c, :], xk[c], xbuf[:, c, :], x1r, x1r[:], ALU.add)
            P.dma("sp", out_dram, out_dram[out_rows + c * 128:out_rows + (c + 1) * 128, :], xk[c], xbuf[:, c, :])

    def phase3():
        ph = Phase()
        ph.wpool(3)
        xbuf = ph.sb("xbuf", [128, 4, D], F32)
        hT_t = ph.sb("hTcat", [128, 16, 512], BF16)
        hTk = [Buf("hTk%d" % k) for k in range(16)]
        hT = Group(hTk, hT_t.ap)
        big = ph.sb("big", [128, 32768], BF16)
        aT = big[:].rearrange("p (k t) -> p k t", k=64)
        BaT = [Buf("aT%d" % k) for k in range(64)]

        def fview(i):
            return big[:, i * 8192:(i + 1) * 8192].bitcast(F32).rearrange("p (k t) -> p k t", k=8)
        yv, uv, gv, ov = fview(0), fview(1), fview(2), fview(3)
        By, Bu_, Bg, Bo = Group(BaT[0:16]), Group(BaT[16:32]), Group(BaT[32:48]), Group(BaT[48:64])
        rv = big[:, 0:16384].bitcast(F32).rearrange("p (c d) -> p c d", c=4)
        y2b = ph.sb("y2b", [128, 8, 512], BF16)
        x1r = ph.sb("x1r", [128, D], F32)
        GG = ph.sb("GG", [128, D], F32)
        tb = [ph.sb("tb%d" % i, [128, 512], F32) for i in range(4)]
        bufs = {"junk": ph.sb("junk", [128, D], BF16), "ss": ph.sb("ss", [128, 4], F32), "hT": hT, "hTk": hTk,
                "aT": aT, "BaT": BaT, "tb": tb, "x1r": [x1r]}
        junk, ssb = bufs["junk"], bufs["ss"]
        glub = ph.sb("glub", [128, 8], F32)
        dsk = ph.sb("dsk", [128, 8], F32)
        P.dma("sp", glub, glub[:], glub_col, glub_col[:])
        P.dma("act", dsk, dsk[:], dskip_col, dskip_col[:])
        for ti, (srcname, r0, nch, full, tt0) in enumerate(tiles[:int(os.environ.get('K_P3_TILES', '99'))]):
            if not full:
                continue
            T = nch * 128
            src = 1 if srcname == "ctx" else 0
            xin = ctx_in if srcname == "ctx" else x_in
            fm = lambda dr: dr[:, tt0:tt0 + T].rearrange("(k p) t -> p k t", p=128)
            P.dma(P.ldq(), xbuf, xbuf[:, 0:nch, :], xin, xin[r0:r0 + T, :].rearrange("(c p) d -> p c d", p=128))
            P.dma(P.ldq(), By, yv[:, :, 0:T], YA, fm(YA))
            P.dma(P.ldq(), Bg, gv[:, :, 0:T], YB, fm(YB))
            P.dma(P.ldq(), Bu_, uv[:, :, 0:T], UTf, fm(UTf))
            P.dma(P.ldq(), Bo, ov[:, :, 0:T], OT, fm(OT))
            P.dma("sp", GG, GG[:], GGS, GGS[0, 0, src].partition_broadcast(128))
            P.tt("dve", By, yv[:, :, 0:T], By, yv[:, :, 0:T], Bg, gv[:, :, 0:T], ALU.add)
            for kt in range(8):
                P.stt(By, yv[:, kt, 0:T], Bu_, uv[:, kt, 0:T], dsk[:, kt:kt + 1], By, yv[:, kt, 0:T], ALU.mult, ALU.add, extra_r=[dsk])
            P.dma(P.ldq(), Bg, gv[:, :, 0:T], GT, fm(GT))
            P.tt("pool", Bu_, uv[:, :, 0:T], By, yv[:, :, 0:T], By, yv[:, :, 0:T], ALU.mult)
            P.ts("pool", Bu_, uv[:, :, 0:T], Bu_, uv[:, :, 0:T], 0.044715, 1.0, ALU.mult, ALU.add)
            P.tt("pool", Bu_, uv[:, :, 0:T], Bu_, uv[:, :, 0:T], By, yv[:, :, 0:T], ALU.mult)
            P.act(Bu_, uv[:, :, 0:T], Bu_, uv[:, :, 0:T], AF.Sigmoid, scale=1.5957691216057308)
            P.tt("dve", By, yv[:, :, 0:T], By, yv[:, :, 0:T], Bu_, uv[:, :, 0:T], ALU.mult)
            P.copy("pool", y2b, y2b[:, :, 0:T], By, yv[:, :, 0:T])
            k = 0
            for cb in range(2):
                wb = wload(glu_w[:, cb * 512:(cb + 1) * 512].rearrange("(kc p) n -> p kc n", p=128))
                for m in range(4):
                    pp = pb[k % 4]
                    k += 1
                    mo = cb * 4 + m

                    def emit(wb=wb, m=m, pp=pp):
                        last = None
                        for kc in range(8):
                            last = nc.tensor.matmul(pp[:, 0:T], lhsT=wb[:, kc, m * 128:(m + 1) * 128], rhs=y2b[:, kc, 0:T],
                                                    start=(kc == 0), stop=(kc == 7))
                        return last
                    P.mm(pp, [wb, y2b], emit)
                    P.act(Bu_, uv[:, mo, 0:T], pp, pp[:, 0:T], AF.Sigmoid, bias=glub[:, mo:mo + 1], extra_r=[glub])
                    P.tt("dve", hT, hT[:, mo, 0:T], By, yv[:, mo, 0:T], Bu_, uv[:, mo, 0:T], ALU.mult)
            P.act(y2b, y2b[:, :, 0:T], Bo, ov[:, :, 0:T], AF.Square)
            for h in range(4):
                pp = pb[4 + (h % 2)]

                def emit(h=h, pp=pp):
                    last = None
                    for ec in range(2):
                        last = nc.tensor.matmul(pp[:, 0:T], lhsT=ones_b[:, :], rhs=y2b[:, 2 * h + ec, 0:T], start=(ec == 0), stop=(ec == 1))
                    return last
                P.mm(pp, [ones_b, y2b], emit)
                P.ts("dve", Bu_, uv[:, h, 0:T], pp, pp[:, 0:T], 1.0 / 256, EPS, ALU.mult, ALU.add)
            P.act(Bu_, uv[:, 0:4, 0:T], Bu_, uv[:, 0:4, 0:T], AF.Sqrt)
            C.op("dve", [Bu_], [Bu_], lambda: nc.vector.reciprocal(out=uv[:, 0:4, 0:T], in_=uv[:, 0:4, 0:T]))
            P.tt("dve", Bo, ov[:, :, 0:T].rearrange("p (h c) t -> p h c t", h=4), Bo, ov[:, :, 0:T].rearrange("p (h c) t -> p h c t", h=4),
                 Bu_, uv[:, 0:4, 0:T].unsqueeze(2).broadcast_to([128, 4, 2, T]), ALU.mult)
            P.act(Bg, gv[:, :, 0:T], Bg, gv[:, :, 0:T], AF.Silu)
            P.tt("pool", hT, hT[:, 8:16, 0:T], Bo, ov[:, :, 0:T], Bg, gv[:, :, 0:T], ALU.mult)
            ev = 0
            for cb in range(4):
                wb = wload(w_out0[:, cb * 512:(cb + 1) * 512].rearrange("(kc p) n -> p kc n", p=128))
                for c in range(nch):
                    pp = pb[c]

                    def emit(wb=wb, c=c, pp=pp):
                        last = None
                        for kc in range(16):
                            last = nc.tensor.matmul(pp[:, :], lhsT=hT[:, kc, c * 128:(c + 1) * 128], rhs=wb[:, kc, :],
                                                    start=(kc == 0), stop=(kc == 15))
                        return last
                    P.mm(pp, [wb, hT], emit)
                    C.op("act" if ev % 2 else "dve", [pp], [By, Bu_],
                         (lambda c=c, cb=cb, pp=pp: nc.scalar.copy(out=rv[:, c, cb * 512:(cb + 1) * 512], in_=pp[:, :])) if ev % 2 else
                         (lambda c=c, cb=cb, pp=pp: nc.vector.tensor_copy(out=rv[:, c, cb * 512:(cb + 1) * 512], in_=pp[:, :])))
                    ev += 1
            row_rstd(junk, ssb, By, lambda c: rv[:, c, :], nch, D)
            for c in range(nch):
                C.op("dve", [By, Bu_, ssb, GG], [By, Bu_], lambda c=c: nc.vector.scalar_tensor_tensor(
                    out=rv[:, c, :], in0=rv[:, c, :], scalar=ssb[:, c:c + 1], in1=GG[:], op0=ALU.mult, op1=ALU.mult))
                C.op("dve", [By, Bu_, xbuf], [xbuf], lambda c=c: nc.vector.tensor_tensor(out=xbuf[:, c, :], in0=xbuf[:, c, :], in1=rv[:, c, :], op=ALU.add))
            P.dma("sp", X1a, X1a[tt0:tt0 + T, :].rearrange("(c p) d -> p c d", p=128), xbuf, xbuf[:, 0:nch, :])
            mlp_tile(bufs, xbuf, nch, 0, src, (X1a, tt0), X1, tt0, GG, GGS[0, 1, src])
        ph.close()

    if not os.environ.get('K_SKIP_P3'):
        phase3()
    if stop_after == "p3":
        return finish(P)

    sink_col = P.din("sink_col", [128, 16])
    rope_in = P.din("rope_tab", [4, 128, FULL_LAT])
    perm_in = P.din("rope_perm", [128, 128])
    mask_in = P.din("attn_mask", [2, 128, 128])
    Q1T = P.dscr("Q1T", [2048, OWN], BF16)
    K1T = P.dscr("K1T", [1024, NFULL], BF16)
    V1 = P.dscr("V1", [NFULL, 1024], BF16)
    X2a = P.dscr("X2a", [OWN, D], F32)
    out_d = P.dout("out", [OWN, D], F32)

    def phase4a():
        ph = Phase()
        ph.wpool(4)
        xt = ph.sb("xt", [128, 4, D], F32)
        bufs = {"junk": ph.sb("junk", [128, D], BF16), "ss": ph.sb("ss", [128, 4], F32)}
        hT = ph.sb("hT", [128, 16, 512], BF16)
        hTk = [Buf("hTk%d" % k) for k in range(16)]
        bufs["hT"], bufs["hTk"] = hT, hTk
        perm = ph.sb("perm", [128, 128], BF16)
        P.dma("pool", perm, perm[:], perm_in, perm_in[:])
        rt = ph.sb("rt", [128, 4, 512], F32)
        st_b = [ph.sb("stb%d" % i, [128, 4, 512], BF16) for i in range(3)]
        qraw = [ph.sb("qraw%d" % i, [128, 512], BF16) for i in range(2)]
        t1s = [ph.sb("t1_%d" % i, [128, 512], F32) for i in range(2)]
        t2s = [ph.sb("t2_%d" % i, [128, 512], F32) for i in range(2)]
        cnt = {"ps": 0, "sb": 0, "ev": 0, "r": 0}
        for ti, (srcname, r0, nch, full, tt0) in enumerate(tiles):
            if not full:
                continue
            T = nch * 128
            src = 1 if srcname == "ctx" else 0
            lat0 = tt0 - NCTX
            P.dma(P.ldq(), xt, xt[:, 0:nch, :], X1, X1[tt0:tt0 + T, :].rearrange("(c p) d -> p c d", p=128))
            norm_to_hT(bufs, xt, nch, 1, src, 0)
            if src == 0:
                P.dma(P.ldq(), rt, rt[:, :, 0:T], rope_in, rope_in[:, :, lat0:lat0 + T].rearrange("f p t -> p f t"))
            need_q = (src == 0 and lat0 < OWN)
            for cb in range(8):
                kind = "q" if cb < 4 else ("k" if cb < 6 else "v")
                if kind == "q" and not need_q:
                    continue
                wb = wload(w_in1[:, cb * 512:(cb + 1) * 512].rearrange("(kc p) n -> p kc n", p=128))
                sb_ = st_b[cnt["sb"] % 3]
                cnt["sb"] += 1
                if kind in ("q", "k"):
                    for m in range(4):
                        pp = pb[cnt["ps"] % 4]
                        cnt["ps"] += 1

                        def emit(wb=wb, m=m, pp=pp):
                            last = None
                            for kc in range(16):
                                last = nc.tensor.matmul(pp[:, 0:T], lhsT=wb[:, kc, m * 128:(m + 1) * 128], rhs=hT[:, kc, 0:T],
                                                        start=(kc == 0), stop=(kc == 15))
                            return last
                        P.mm(pp, [wb] + hTk, emit)
                        if src == 1:
                            P.copy("act", sb_, sb_[:, m, 0:T], pp, pp[:, 0:T])
                            continue
                        ri = cnt["r"] % 2
                        cnt["r"] += 1
                        qr, t1, t2 = qraw[ri], t1s[ri], t2s[ri]
                        ci, si = (0, 1) if kind == "q" else (2, 3)
                        P.copy("act", qr, qr[:, 0:T], pp, pp[:, 0:T])
                        P.tt("dve", t1, t1[:, 0:T], pp, pp[:, 0:T], rt, rt[:, ci, 0:T], ALU.mult)
                        p2 = pb[4 + (cnt["r"] % 2)]
                        P.mm(p2, [perm, qr], lambda p2=p2, qr=qr: nc.tensor.matmul(p2[:, 0:T], lhsT=perm[:, :], rhs=qr[:, 0:T], start=True, stop=True))
                        P.tt("dve", t2, t2[:, 0:T], p2, p2[:, 0:T], rt, rt[:, si, 0:T], ALU.mult)
                        P.tt("pool", sb_, sb_[:, m, 0:T], t1, t1[:, 0:T], t2, t2[:, 0:T], ALU.add)
                    if kind == "q":
                        rows = slice(cb * 512, (cb + 1) * 512)
                        P.dma(P.ldq(), Q1T, Q1T[rows, lat0:lat0 + T].rearrange("(m p) t -> p m t", p=128), sb_, sb_[:, :, 0:T])
                    else:
                        rows = slice((cb - 4) * 512, (cb - 3) * 512)
                        P.dma(P.ldq(), K1T, K1T[rows, tt0:tt0 + T].rearrange("(m p) t -> p m t", p=128), sb_, sb_[:, :, 0:T])
                else:
                    for c in range(nch):
                        pp = pb[cnt["ps"] % 4]
                        cnt["ps"] += 1

                        def emit(wb=wb, c=c, pp=pp):
                            last = None
                            for kc in range(16):
                                last = nc.tensor.matmul(pp[:, :], lhsT=hT[:, kc, c * 128:(c + 1) * 128], rhs=wb[:, kc, :],
                                                        start=(kc == 0), stop=(kc == 15))
                            return last
                        P.mm(pp, [wb] + hTk, emit)
                        P.copy("act" if cnt["ev"] % 2 else "dve", sb_, sb_[:, c, :], pp, pp[:, :])
                        cnt["ev"] += 1
                    P.dma(P.ldq(), V1, V1[tt0:tt0 + T, (cb - 6) * 512:(cb - 5) * 512].rearrange("(c p) n -> p c n", p=128), sb_, sb_[:, 0:nch, :])
        ph.close()

    phase4a()

    def phase4b():
        ph = Phase()
        wo = ph.sb("wo", [128, 16, D], BF16)
        for cb in range(4):
            P.dma("pool" if w_out1.ap.dtype == F32 else P.ldq(), wo, wo[:, :, cb * 512:(cb + 1) * 512], None,
                  w_out1[:, cb * 512:(cb + 1) * 512].rearrange("(kc p) n -> p kc n", p=128))
        maskb = ph.sb("maskb", [128, 2, 128], BF16)
        P.dma("pool", maskb, maskb[:], mask_in, mask_in[:].rearrange("a k q -> k a q"))
        esink = ph.sb("esink", [128, 16], F32)
        P.dma("sp", esink, esink[:], sink_col, sink_col[:])
        P.act(esink, esink[:], esink, esink[:], AF.Exp)
        Eh = ph.sb("Eh", [128, 2, 128], BF16)
        P.memset("dve", Eh, Eh[:], 0.0)
        P.memset("dve", Eh, Eh[:, 0, 0:64], 1.0)
        P.memset("dve", Eh, Eh[:, 1, 64:128], 1.0)
        Kc = ph.sb("Kc", [128, 8, 256], BF16)
        Vc = ph.sb("Vc", [128, 2, 1024], BF16)
        P.dma("sp", Kc, Kc[:], K1T, K1T[:, 0:256].rearrange("(j p) t -> p j t", p=128))
        P.dma("act", Vc, Vc[:], V1, V1[0:256, :].rearrange("(b p) n -> p b n", p=128))
        GG = ph.sb("GG", [128, D], F32)
        P.dma("sp", GG, GG[:], GGS, GGS[1, 0, 0].partition_broadcast(128))
        Kl = [ph.sb("Kl%d" % i, [128, 8, 384], BF16) for i in range(2)]
        Vl = [ph.sb("Vl%d" % i, [128, 3, 1024], BF16) for i in range(2)]
        qTs = [ph.sb("q1T%d" % i, [128, 16, 128], BF16) for i in range(2)]
        PTs = [ph.sb("PT%d" % i, [128, 512], BF16) for i in range(4)]
        attnT = [ph.sb("attnT%d" % i, [128, 16, 128], BF16) for i in range(2)]
        dtmp = ph.sb("dtmp", [128, 4, 128], F32)
        xc = [ph.sb("xc%d" % i, [128, D], F32) for i in range(2)]
        rsb = ph.sb("rsb", [128, D], F32)
        junk = ph.sb("junk", [128, D], BF16)
        ssb = ph.sb("ss", [128, 4], F32)
        npt = 0
        for c in range(16):
            bi = c % 2
            kl, vl, qT, aT_, xcb = Kl[bi], Vl[bi], qTs[bi], attnT[bi], xc[bi]
            lo_c = max(c - 1, 0)
            nloc = (c + 2 - lo_c)
            r0 = NCTX + lo_c * 128
            P.dma(P.ldq(), kl, kl[:, :, 0:nloc * 128], K1T, K1T[:, r0:r0 + nloc * 128].rearrange("(j p) t -> p j t", p=128))
            P.dma(P.ldq(), vl, vl[:, 0:nloc, :], V1, V1[r0:r0 + nloc * 128, :].rearrange("(b p) n -> p b n", p=128))
            P.dma(P.ldq(), qT, qT[:], Q1T, Q1T[:, c * 128:(c + 1) * 128].rearrange("(j p) t -> p j t", p=128))
            P.dma(P.ldq(), xcb, xcb[:], X1, X1[NCTX + c * 128:NCTX + (c + 1) * 128, :])
            blocks = [("c", 0, None), ("c", 1, None)]
            for lc in range(nloc):
                chunk = lo_c + lc
                mk = 0 if chunk == c - 1 else (1 if chunk == c + 1 else None)
                blocks.append(("l", lc, mk))
            jobs = []
            for kvh in range(4):
                for bi_, (kind, bidx, mk) in enumerate(blocks):
                    for par in range(2):
                        var = kvh * 2 + par
                        if kind == "c":
                            jobs.append((kvh, Kc[:, var, bidx * 128:(bidx + 1) * 128], Vc[:, bidx, var * 128:(var + 1) * 128], Kc, Vc, mk, par,
                                         bi_ == 0 and par == 0, bi_ == len(blocks) - 1 and par == 1))
                        else:
                            jobs.append((kvh, kl[:, var, bidx * 128:(bidx + 1) * 128], vl[:, bidx, var * 128:(var + 1) * 128], kl, vl, mk, par,
                                         bi_ == 0 and par == 0, bi_ == len(blocks) - 1 and par == 1))

            def emit_S(i):
                kvh, k_ap, v_ap, kb_, vb_, mk, par, f, l = jobs[i]
                sp_ = pb[(npt0 + i) % 2]
                q_ap = qT[:, 4 * kvh:4 * kvh + 4, :]

                def emit_s():
                    last = nc.tensor.matmul(sp_[:, :], lhsT=k_ap, rhs=q_ap, start=True, stop=(mk is None))
                    if mk is not None:
                        last = nc.tensor.matmul(sp_[:, :], lhsT=ident_b[:, :], rhs=maskb[:, mk, :].unsqueeze(1).broadcast_to([128, 4, 128]),
                                                start=False, stop=True)
                    return last
                P.mm(sp_, [kb_, qT, ident_b, maskb], emit_s)
            npt0 = npt
            emit_S(0)
            for i in range(len(jobs)):
                kvh, k_ap, v_ap, kb_, vb_, mk, par, f, l = jobs[i]
                if i + 1 < len(jobs):
                    emit_S(i + 1)
                sp_ = pb[(npt0 + i) % 2]
                PT = PTs[(npt0 + i) % 4]
                pv, den = (pb[2], pb[3]) if kvh % 2 == 0 else (pb[4], pb[5])
                P.act(PT, PT[:, :], sp_, sp_[:, :], AF.Exp)
                P.mm(pv, [vb_, PT], lambda v_ap=v_ap, PT=PT, f=f, l=l, pv=pv: nc.tensor.matmul(pv[:, :], lhsT=v_ap, rhs=PT[:, :], start=f, stop=l))
                P.mm(den, [Eh, PT], lambda par=par, PT=PT, f=f, l=l, den=den: nc.tensor.matmul(den[:, :], lhsT=Eh[:, par, :], rhs=PT[:, :], start=f, stop=l))
                if l:
                    P.tt("dve", dtmp, dtmp[:], den, den[:, :].rearrange("p (j n) -> p j n", j=4), esink,
                         esink[:, 4 * kvh:4 * kvh + 4].unsqueeze(2).broadcast_to([128, 4, 128]), ALU.add)
                    C.op("dve", [dtmp], [dtmp], lambda: nc.vector.reciprocal(out=dtmp[:], in_=dtmp[:]))
                    P.tt("dve", aT_, aT_[:, 4 * kvh:4 * kvh + 4, :], pv, pv[:, :].rearrange("p (j n) -> p j n", j=4), dtmp, dtmp[:], ALU.mult)
            npt += len(jobs)
            for cb in range(4):
                pp = pb[6 + cb % 2]

                def emit(cb=cb, pp=pp, aT_=aT_):
                    last = None
                    for kc in range(16):
                        last = nc.tensor.matmul(pp[:, :], lhsT=aT_[:, kc, :], rhs=wo[:, kc, cb * 512:(cb + 1) * 512], start=(kc == 0), stop=(kc == 15))
                    return last
                P.mm(pp, [aT_, wo], emit)
                P.copy("act", rsb, rsb[:, cb * 512:(cb + 1) * 512], pp, pp[:, :])
            row_rstd(junk, ssb, rsb, lambda c_: rsb[:, :], 1, D)
            P.stt(rsb, rsb[:], rsb, rsb[:], ssb[:, 0:1], GG, GG[:], ALU.mult, ALU.mult, extra_r=[ssb])
            P.tt("dve", xcb, xcb[:], xcb, xcb[:], rsb, rsb[:], ALU.add)
            P.dma(P.ldq(), X2a, X2a[c * 128:(c + 1) * 128, :], xcb, xcb[:])
        ph.close()

    phase4b()
    if stop_after == "p4":
        return finish(P)

    def phase5():
        ph = Phase()
        ph.wpool(3)
        xbuf = ph.sb("xbuf", [128, 4, D], F32)
        xk = [Buf("xk%d" % c) for c in range(4)]
        hT = ph.sb("hT5", [128, 16, 512], BF16)
        hTk = [Buf("hTk%d" % k) for k in range(16)]
        big = ph.sb("big5", [128, 32768], BF16)
        aT = big[:].rearrange("p (k t) -> p k t", k=64)
        BaT = [Buf("aT%d" % k) for k in range(64)]
        GG = ph.sb("GG5", [128, D], F32)
        bufs = {"junk": ph.sb("junk", [128, D], BF16), "ss": ph.sb("ss", [128, 4], F32), "hT": hT, "hTk": hTk,
                "aT": aT, "BaT": BaT,
                "tb": [ph.sb("tb%d" % i, [128, 512], F32) for i in range(4)],
                "x1r": [ph.sb("x1r%d" % i, [128, D], F32) for i in range(2)]}
        for t0 in range(0, OWN, 512):
            for c in range(4):
                P.dma(P.ldq(), xk[c], xbuf[:, c, :], X2a, X2a[t0 + c * 128:t0 + (c + 1) * 128, :])
            mlp_tile(bufs, xbuf, 4, 1, 0, (X2a, t0), out_d, t0, GG, GGS[1, 1, 0], xk=xk)
        ph.close()

    phase5()
    return finish(P)


def finish(P):
    P.C.barrier()
    return P


def col16(v):
    return np.ascontiguousarray(np.asarray(v).reshape(16, 128).T)


def host_inputs(core, inp):
    b, s = core // 2, core % 2
    rev = (s == 1)
    x = inp["x"][b]
    ctx = inp["ctx"][b]
    if rev:
        x = x[::-1]
        ctx = ctx[::-1]
    m = {}
    m["x_loc"] = np.ascontiguousarray(x, dtype=np.float32)
    m["ctx_loc"] = np.ascontiguousarray(ctx, dtype=np.float32)
    m["c_col"] = np.ascontiguousarray(np.stack([col16(inp["c"][b]), col16(inp["c_ctx"])], axis=1))
    m["mod_w"] = inp["mod_w"]
    mb = inp["mod_b"].reshape(2, 6, D)
    m["modb_col"] = np.ascontiguousarray(np.stack([np.stack([col16(mb[i, j]) for j in (0, 1, 3, 4)], axis=1) for i in range(2)], axis=1))
    m["modb_row"] = np.ascontiguousarray(mb[:, [2, 5], :])
    ng = inp["norm_g"]
    m["g_col"] = np.ascontiguousarray(np.stack([np.stack([col16(ng[i, j]) for j in (0, 2)], axis=1) for i in range(2)], axis=1))
    m["g_row"] = np.ascontiguousarray(ng[:, [1, 3], :])
    m["ident"] = np.eye(128, dtype=np.float32)
    m["even_w_in"] = inp["even_w_in"][0]
    m["even_w_out"] = inp["even_w_out"][0]
    m["glu_w"] = inp["s5_glu_w"][0]
    m["glub_col"] = np.ascontiguousarray(inp["s5_glu_b"][0].reshape(8, 128).T)
    m["dskip_col"] = np.ascontiguousarray(inp["s5_d"][0].reshape(8, 128).T)
    m["mlp_w1"] = inp["mlp_w1"]
    m["mlp_w2"] = inp["mlp_w2"]
    w1 = inp["odd_w_in"][0]
    ext = np.zeros((D, 4096), np.float32)
    ext[:, :2048] = w1[:, :2048]
    for kv in range(4):
        kcols = w1[:, 2048 + kv * 64: 2048 + (kv + 1) * 64]
        vcols = w1[:, 2304 + kv * 64: 2304 + (kv + 1) * 64]
        for par in range(2):
            base = (kv * 2 + par) * 128 + par * 64
            ext[:, 2048 + base: 2048 + base + 64] = kcols
            ext[:, 3072 + base: 3072 + base + 64] = vcols
    m["odd_w_in_ext"] = ext
    m["odd_w_out"] = inp["odd_w_out"][0]
    sk = inp["odd_sink"][0]
    m["sink_col"] = np.ascontiguousarray(np.stack([np.where(np.arange(128) < 64, sk[2 * c2], sk[2 * c2 + 1]) for c2 in range(16)], axis=1).astype(np.float32))
    tloc = np.arange(FULL_LAT)
    pos = np.where(tloc < L, tloc, 0)
    pos = (L - 1 - pos) if rev else pos
    row = (pos // 64).astype(np.float64); colp = (pos % 64).astype(np.float64)
    inv_freq = 10000.0 ** (-np.arange(16, dtype=np.float64) / 16)
    ang = np.concatenate([row[:, None] * inv_freq[None], colp[:, None] * inv_freq[None]], axis=-1)
    dd = np.arange(128) % 64
    cosT = np.cos(ang[:, dd % 32]).T
    sinT = np.sin(ang[:, dd % 32]).T * np.where(dd < 32, -1.0, 1.0)[:, None]
    m["rope_tab"] = np.stack([cosT * 0.125, sinT * 0.125, cosT, sinT]).astype(np.float32)
    perm = np.zeros((128, 128), np.float32)
    for mm_ in range(128):
        src_ = mm_ + 32 if (mm_ % 64) < 32 else mm_ - 32
        perm[src_, mm_] = 1.0
    m["rope_perm"] = perm
    kk, qq = np.meshgrid(np.arange(128), np.arange(128), indexing="ij")
    m["attn_mask"] = np.stack([np.where(kk >= qq, 0.0, -30000.0), np.where(kk <= qq, 0.0, -30000.0)]).astype(np.float32)
    T = 128
    pos = np.arange(T, dtype=np.float64)
    dmask = np.zeros((128, 2, 4, 128), np.float64)
    qdec = np.zeros((2, 4, 128), np.float64)
    kdec = np.zeros((128, 2, 4), np.float64)
    cdec = np.zeros((128, 2, 4), np.float64)
    for X in range(2):
        d_idx = X if s == 0 else 1 - X
        for h in range(4):
            lg = np.log1p(-np.exp2(-5.0 - (2.0 * h + d_idx)))
            mm, nn = np.meshgrid(pos, pos, indexing="ij")
            if X == 0:
                dmask[:, X, h, :] = np.where(nn >= mm, np.exp((nn - mm) * lg), 0.0) * 256 ** -0.5
                qdec[X, h] = np.exp((pos + 1) * lg)
                kdec[:, X, h] = np.exp((T - 1 - pos) * lg) * 256 ** -0.5
            else:
                dmask[:, X, h, :] = np.where(mm >= nn, np.exp((mm - nn) * lg), 0.0) * 256 ** -0.5
                qdec[X, h] = np.exp((T - pos) * lg)
                kdec[:, X, h] = np.exp(pos * lg) * 256 ** -0.5
            cdec[:, X, h] = np.exp(T * lg)
    lre = np.zeros((128, 64), np.float32); lim = np.zeros((128, 64), np.float32); ldt = np.zeros((128, 64), np.float32)
    Bc = np.zeros((128, 64, 2, 32), np.float32); Cc = np.zeros((128, 64, 2, 32), np.float32)
    for X in range(2):
        d_idx = X if s == 0 else 1 - X
        for g2 in range(2):
            rows = slice(g2 * 64, g2 * 64 + 64)
            cols = slice(X * 32, X * 32 + 32)
            gsel = np.arange(32) * 2 + g2
            lre[rows, cols] = inp["s5_lam_re"][0, d_idx, gsel, :].T
            lim[rows, cols] = inp["s5_lam_im"][0, d_idx, gsel, :].T
            ldt[rows, cols] = inp["s5_log_dt"][0, d_idx, gsel][None, :]
            qs = slice(g2 * 16, g2 * 16 + 16)
            Bc[rows, cols, 0, qs] = inp["s5_b_re"][0, d_idx, gsel].transpose(1, 0, 2)
            Bc[rows, cols, 1, qs] = inp["s5_b_im"][0, d_idx, gsel].transpose(1, 0, 2)
            Cc[rows, cols, 0, qs] = inp["s5_c_re"][0, d_idx, gsel].transpose(2, 0, 1)
            Cc[rows, cols, 1, qs] = inp["s5_c_im"][0, d_idx, gsel].transpose(2, 0, 1)
    m["s5_lre"] = lre; m["s5_lim"] = lim; m["s5_ldt"] = ldt; m["s5_B"] = Bc; m["s5_C"] = Cc
    m["ret_dmask"] = dmask.astype(np.float32)
    m["ret_qdec"] = qdec.astype(np.float32)
    m["ret_kdec"] = kdec.astype(np.float32)
    m["ret_cdec"] = cdec.astype(np.float32)
    return m


_CACHE = {}


def kernel(**inputs):
    inp = {k: np.asarray(v) for k, v in inputs.items()}
    if "prog" not in _CACHE:
        _CACHE["prog"] = build()
    P = _CACHE["prog"]
    maps = []
    for core in range(8):
        hm = host_inputs(core, inp)
        maps.append({k: hm[k] for k in P.inputs})
    res = run_bass_kernel_spmd(P.nc, maps, core_ids=list(range(8)))
    out = np.zeros((4, L, D), np.float32)
    for core in range(8):
        b, s = core // 2, core % 2
        o = np.asarray(res.results[core]["out"])
        if s == 0:
            out[b, :OWN] = o
        else:
            out[b, OWN:] = o[::-1]
    return out
```
